# Optimizing a Trainium2 kernel written in Bass

```python
import math
import jax
import jax.numpy as jnp
from jax import lax
import numpy as np

D_MODEL = 2048
BATCH = 16
SEQ = 2048
DEPTH = 4

GRID_W = 64
CTX_LEN = 256

NA_HEADS = 8
NA_HEAD_DIM = 128
NA_WIN_ROWS = 8
NA_WIN_COLS = 16
ROPE_THETA = 10000.0

FN_GROUPS = 4
FN_GROUP_DIM = 256

SSM_HEADS = 32
SSM_HEAD_DIM = 64
SSM_INNER = SSM_HEADS * SSM_HEAD_DIM
SSM_GROUPS = 4
SSM_STATE = 128
SSM_CONV = 5
SSM_CHUNK = 128

FFN_HIDDEN = -(-8 * D_MODEL // (3 * 256)) * 256

NA_WIDTH = NA_HEADS * NA_HEAD_DIM
FN_WIDTH = FN_GROUPS * FN_GROUP_DIM
CONV_DIM = SSM_INNER + 2 * SSM_GROUPS * SSM_STATE
PROJ_WIDTH = 3 * NA_WIDTH + FN_WIDTH + CONV_DIM + SSM_INNER + 2 * SSM_HEADS + 3 * D_MODEL

kernel_name = 'hybrid_natten_fnet_ssd_dit_trunk'


def _layer_norm(x, g=None, b=None, eps=1e-6):
    xf = x.astype(jnp.float32)
    mu = jnp.mean(xf, axis=-1, keepdims=True)
    var = jnp.mean(jnp.square(xf - mu), axis=-1, keepdims=True)
    y = (xf - mu) * lax.rsqrt(var + eps)
    if g is not None:
        y = y * g.astype(jnp.float32) + b.astype(jnp.float32)
    return y.astype(x.dtype)


def _modulate(x, shift, scale):
    return _layer_norm(x) * (1 + scale) + shift


def _split_proj(p):
    sizes = (NA_WIDTH, NA_WIDTH, NA_WIDTH, FN_WIDTH, CONV_DIM, SSM_INNER, 2 * SSM_HEADS)
    cuts = np.cumsum(sizes).tolist()
    return jnp.split(p, cuts, axis=-1)


def _axial_rope(n_tokens):
    t = jnp.arange(n_tokens)
    row = (t // GRID_W).astype(jnp.float32)
    col = (t % GRID_W).astype(jnp.float32)
    n_freq = NA_HEAD_DIM // 4
    inv = ROPE_THETA ** (-jnp.arange(n_freq, dtype=jnp.float32) / n_freq)
    ang = jnp.stack([row[:, None] * inv, col[:, None] * inv], axis=1)
    return jnp.cos(ang), jnp.sin(ang)


def _apply_rope(t, cos, sin):
    b, s, h, d = t.shape
    tf = t.astype(jnp.float32).reshape(b, s, h, 2, 2, d // 4)
    cs = cos[None, :, None]
    sn = sin[None, :, None]
    t1, t2 = tf[..., 0, :], tf[..., 1, :]
    out = jnp.stack([t1 * cs - t2 * sn, t2 * cs + t1 * sn], axis=-2)
    return out.reshape(b, s, h, d).astype(t.dtype)


def _natten_latent(q_rot, q_raw, k_rot, v, k_ctx, v_ctx, rpb):
    b, s, h, d = q_rot.shape
    rows = s // GRID_W
    kr = min(NA_WIN_ROWS, rows)
    kc = NA_WIN_COLS
    scale = d ** -0.5
    grid = lambda t: t.reshape(b, rows, GRID_W, h, d)
    k_g, v_g = grid(k_rot), grid(v)
    qr_rows = jnp.moveaxis(grid(q_rot), 1, 0)
    qp_rows = jnp.moveaxis(grid(q_raw), 1, 0)
    r_idx = jnp.arange(rows)
    r_start = jnp.clip(r_idx - kr // 2, 0, rows - kr)
    col = jnp.arange(GRID_W)
    c_start = jnp.clip(col - kc // 2, 0, GRID_W - kc)
    col_in = (col[None, :] >= c_start[:, None]) & (col[None, :] < c_start[:, None] + kc)
    dc = jnp.clip(col[None, :] - col[:, None], -(kc - 1), kc - 1) + (NA_WIN_COLS - 1)

    def row_block(xs):
        qr, qp, r, rs = xs
        k_blk = lax.dynamic_slice_in_dim(k_g, rs, kr, axis=1)
        v_blk = lax.dynamic_slice_in_dim(v_g, rs, kr, axis=1)
        s_lat = jnp.einsum('bqhd,bikhd->bhqik', qr, k_blk).astype(jnp.float32) * scale
        dr = rs + jnp.arange(kr) - r + (NA_WIN_ROWS - 1)
        bias = rpb[:, dr[None, :, None], dc[:, None, :]].astype(jnp.float32)
        s_lat = jnp.where(col_in[:, None, :], s_lat + bias, -jnp.inf)
        s_ctx = jnp.einsum('bqhd,bchd->bhqc', qp, k_ctx).astype(jnp.float32) * scale
        logits = jnp.concatenate([s_lat.reshape(b, h, GRID_W, kr * GRID_W), s_ctx], axis=-1)
        p = jax.nn.softmax(logits, axis=-1)
        p_lat = p[..., :kr * GRID_W].reshape(b, h, GRID_W, kr, GRID_W).astype(v.dtype)
        p_ctx = p[..., kr * GRID_W:].astype(v.dtype)
        return (jnp.einsum('bhqik,bikhd->bqhd', p_lat, v_blk)
                + jnp.einsum('bhqc,bchd->bqhd', p_ctx, v_ctx))

    out = lax.map(row_block, (qr_rows, qp_rows, r_idx, r_start))
    return jnp.moveaxis(out, 0, 1).reshape(b, s, h * d)


def _ctx_attention(q, k, v):
    b, n, h, d = q.shape
    s = jnp.einsum('bqhd,bkhd->bhqk', q, k).astype(jnp.float32) * (d ** -0.5)
    p = jax.nn.softmax(s, axis=-1).astype(v.dtype)
    return jnp.einsum('bhqk,bkhd->bqhd', p, v).reshape(b, n, h * d)


def _fourier_mix(u):
    b, s, _ = u.shape
    ug = u.astype(jnp.float32).reshape(b, s, FN_GROUPS, FN_GROUP_DIM)
    f = jnp.fft.fft2(ug, axes=(1, 3), norm='ortho').real
    return f.reshape(b, s, FN_WIDTH).astype(u.dtype)


def _dw_conv(x, w, bias):
    out = lax.conv_general_dilated(
        x, w[:, None, :].astype(x.dtype), window_strides=(1,),
        padding=[(SSM_CONV // 2, SSM_CONV // 2)],
        dimension_numbers=('NWC', 'WIO', 'NWC'), feature_group_count=x.shape[-1])
    return out + bias.astype(x.dtype)


def _ssd_prep(xbc, dt_raw, conv_w, conv_b, dt_bias):
    xbc = jax.nn.silu(_dw_conv(xbc, conv_w, conv_b))
    b, l, _ = xbc.shape
    xs, bm, cm = jnp.split(xbc, [SSM_INNER, SSM_INNER + SSM_GROUPS * SSM_STATE], axis=-1)
    xs = xs.reshape(b, l, SSM_HEADS, SSM_HEAD_DIM)
    bm = bm.reshape(b, l, SSM_GROUPS, SSM_STATE)
    cm = cm.reshape(b, l, SSM_GROUPS, SSM_STATE)
    dt = jax.nn.softplus(dt_raw.astype(jnp.float32) + dt_bias.reshape(-1).astype(jnp.float32))
    return xs, bm, cm, dt.reshape(b, l, 2, SSM_HEADS)


def _segsum(a):
    t = a.shape[-1]
    ax = jnp.broadcast_to(a[..., :, None], a.shape + (t,))
    ax = jnp.where(jnp.tril(jnp.ones((t, t), bool), -1), ax, 0)
    seg = jnp.cumsum(ax, axis=-2)
    return jnp.where(jnp.tril(jnp.ones((t, t), bool), 0), seg, -jnp.inf)


def _ssd_chunked(x, dt, A, bm, cm, h0):
    b, l, h, p = x.shape
    g, n = bm.shape[2], bm.shape[3]
    r = h // g
    q = SSM_CHUNK
    nc = l // q
    xd = (x * dt[..., None]).reshape(b, nc, q, g, r, p)
    a = jnp.transpose((dt * A).reshape(b, nc, q, g, r), (0, 3, 4, 1, 2))
    bc = bm.reshape(b, nc, q, g, n)
    cc = cm.reshape(b, nc, q, g, n)
    a_cum = jnp.cumsum(a, axis=-1)
    lmat = jnp.exp(_segsum(a))
    cb = jnp.einsum('bclgn,bcsgn->bgcls', cc, bc)
    y_diag = jnp.einsum('bgcls,bgrcls,bcsgrp->bclgrp', cb, lmat, xd)
    decay_states = jnp.exp(a_cum[..., -1:] - a_cum)
    states = jnp.einsum('bcsgn,bgrcs,bcsgrp->bcgrpn', bc, decay_states, xd)
    states = jnp.concatenate([h0.reshape(b, 1, g, r, p, n).astype(states.dtype), states], axis=1)
    chunk_a = jnp.pad(a_cum[..., -1], ((0, 0), (0, 0), (0, 0), (1, 0)))
    decay_chunk = jnp.exp(_segsum(chunk_a))
    new_states = jnp.einsum('bgrzc,bcgrpn->bzgrpn', decay_chunk, states)
    y_off = jnp.einsum('bclgn,bcgrpn,bgrcl->bclgrp', cc, new_states[:, :-1], jnp.exp(a_cum))
    y = (y_diag + y_off).reshape(b, l, h, p)
    return y, new_states[:, -1].reshape(b, h, p, n)


def _ssd_final_state(x, dt, A, bm):
    b, l, h, p = x.shape
    a_cum = jnp.cumsum(dt * A, axis=1)
    w = dt * jnp.exp(a_cum[:, -1:] - a_cum)
    xd = (x * w[..., None]).reshape(b, l, SSM_GROUPS, h // SSM_GROUPS, p)
    return jnp.einsum('blgn,blgrp->bgrpn', bm, xd).reshape(b, h, p, SSM_STATE)


def _bi_ssd(xs, bm, cm, dt, A, h0_f, h0_b):
    rev = lambda t: t[:, ::-1]
    y_f, h_f = _ssd_chunked(xs, dt[:, :, 0], A[0], bm, cm, h0_f)
    y_b, h_b = _ssd_chunked(rev(xs), rev(dt[:, :, 1]), A[1], rev(bm), rev(cm), h0_b)
    return y_f + rev(y_b), h_f, h_b


def _ssd_out(y, xs, z, d_skip, norm_w):
    b, l, h, p = xs.shape
    y = (y + d_skip[:, None] * xs).reshape(b, l, SSM_INNER)
    gy = (y * jax.nn.silu(z.astype(jnp.float32))).astype(jnp.float32).reshape(b, l, SSM_GROUPS, -1)
    gy = gy * lax.rsqrt(jnp.mean(jnp.square(gy), axis=-1, keepdims=True) + 1e-5)
    return (gy.reshape(b, l, SSM_INNER) * norm_w.astype(jnp.float32)).astype(xs.dtype)


def _merge(o_na, o_fn, o_ssm, gate_logits, w_br_na, w_br_fn, w_br_ssm, w_out):
    g_na, g_fn, g_ssm = jnp.split(jax.nn.sigmoid(gate_logits), 3, axis=-1)
    m = g_na * (o_na @ w_br_na) + g_fn * (o_fn @ w_br_fn) + g_ssm * (o_ssm @ w_br_ssm)
    return m @ w_out


def _swiglu(h, w_gate, w_up, w_down):
    return (jax.nn.silu(h @ w_gate) * (h @ w_up)) @ w_down


def _mixers(hx, hc, rope, w_in, rpb, conv_w, conv_b, a_log, dt_bias, d_skip, norm_w,
            w_br_na, w_br_fn, w_br_ssm, w_out, with_ctx_out):
    b = hx.shape[0]
    cos, sin = rope
    heads = lambda t: t.reshape(t.shape[0], t.shape[1], NA_HEADS, NA_HEAD_DIM)
    qx, kx, vx, ux, xbcx, zx, dtx, gx = _split_proj(hx @ w_in)
    qc, kc, vc, uc, xbcc, zc, dtc, gc = _split_proj(hc @ w_in)
    k_ctx, v_ctx = heads(kc), heads(vc)
    qx_h = heads(qx)
    o_na = _natten_latent(_apply_rope(qx_h, cos, sin), qx_h, _apply_rope(heads(kx), cos, sin),
                          heads(vx), k_ctx, v_ctx, rpb)
    o_fn = _fourier_mix(ux)
    A = -jnp.exp(a_log.astype(jnp.float32))
    xs_c, b_c, c_c, dt_c = _ssd_prep(xbcc, dtc, conv_w, conv_b, dt_bias)
    if with_ctx_out:
        zeros = jnp.zeros((b, SSM_HEADS, SSM_HEAD_DIM, SSM_STATE), jnp.float32)
        y_c, h_f, h_b = _bi_ssd(xs_c, b_c, c_c, dt_c, A, zeros, zeros)
    else:
        h_f = _ssd_final_state(xs_c, dt_c[:, :, 0], A[0], b_c)
        h_b = _ssd_final_state(xs_c[:, ::-1], dt_c[:, ::-1, 1], A[1], b_c[:, ::-1])
    xs_x, b_x, c_x, dt_x = _ssd_prep(xbcx, dtx, conv_w, conv_b, dt_bias)
    y_x, _, _ = _bi_ssd(xs_x, b_x, c_x, dt_x, A, h_f, h_b)
    o_ssm = _ssd_out(y_x, xs_x, zx, d_skip, norm_w)
    mix_x = _merge(o_na, o_fn, o_ssm, gx, w_br_na, w_br_fn, w_br_ssm, w_out)
    if not with_ctx_out:
        return mix_x, None
    o_na_c = _ctx_attention(heads(qc), k_ctx, v_ctx)
    o_fn_c = _fourier_mix(uc)
    o_ssm_c = _ssd_out(y_c, xs_c, zc, d_skip, norm_w)
    mix_c = _merge(o_na_c, o_fn_c, o_ssm_c, gc, w_br_na, w_br_fn, w_br_ssm, w_out)
    return mix_x, mix_c


def setup_inputs(seed: int = 0) -> dict:
    key = jax.random.key(seed)
    ks = jax.random.split(key, 24)
    f32 = jnp.float32
    nrm = lambda k, shape, s: s * jax.random.normal(k, shape, f32)
    beta = (8.0 * DEPTH) ** -0.25
    dt0 = jnp.exp(jax.random.uniform(ks[10], (DEPTH, 2, SSM_HEADS), f32, math.log(1e-3), math.log(1e-1)))
    return {
        'x': nrm(ks[0], (BATCH, SEQ, D_MODEL), 1.0),
        'c': nrm(ks[1], (BATCH, D_MODEL), 1.0),
        'ctx': nrm(ks[2], (BATCH, CTX_LEN, D_MODEL), 1.0),
        'c_ctx': nrm(ks[3], (D_MODEL,), 1.0),
        'w_mod': nrm(ks[4], (DEPTH, D_MODEL, 6 * D_MODEL), 0.5 * D_MODEL ** -0.5),
        'w_in': nrm(ks[5], (DEPTH, D_MODEL, PROJ_WIDTH), D_MODEL ** -0.5),
        'na_rpb': nrm(ks[6], (DEPTH, NA_HEADS, 2 * NA_WIN_ROWS - 1, 2 * NA_WIN_COLS - 1), 0.02),
        'ssm_conv_w': nrm(ks[7], (DEPTH, SSM_CONV, CONV_DIM), SSM_CONV ** -0.5),
        'ssm_conv_b': nrm(ks[8], (DEPTH, CONV_DIM), 0.01),
        'ssm_a_log': jnp.log(jax.random.uniform(ks[9], (DEPTH, 2, SSM_HEADS), f32, 1.0, 16.0)),
        'ssm_dt_bias': dt0 + jnp.log(-jnp.expm1(-dt0)),
        'ssm_d': 1.0 + nrm(ks[11], (DEPTH, SSM_HEADS), 0.1),
        'ssm_norm_w': 1.0 + nrm(ks[12], (DEPTH, SSM_INNER), 0.1),
        'w_br_na': nrm(ks[13], (DEPTH, NA_WIDTH, D_MODEL), NA_WIDTH ** -0.5),
        'w_br_fn': nrm(ks[14], (DEPTH, FN_WIDTH, D_MODEL), FN_WIDTH ** -0.5),
        'w_br_ssm': nrm(ks[15], (DEPTH, SSM_INNER, D_MODEL), SSM_INNER ** -0.5),
        'w_out': nrm(ks[16], (DEPTH, D_MODEL, D_MODEL), beta * D_MODEL ** -0.5),
        'ln1_g': 1.0 + nrm(ks[17], (DEPTH, D_MODEL), 0.1),
        'ln1_b': nrm(ks[18], (DEPTH, D_MODEL), 0.01),
        'w_ffn_gate': nrm(ks[19], (DEPTH, D_MODEL, FFN_HIDDEN), D_MODEL ** -0.5),
        'w_ffn_up': nrm(ks[20], (DEPTH, D_MODEL, FFN_HIDDEN), D_MODEL ** -0.5),
        'w_ffn_down': nrm(ks[21], (DEPTH, FFN_HIDDEN, D_MODEL), beta * FFN_HIDDEN ** -0.5),
        'ln2_g': 1.0 + nrm(ks[22], (DEPTH, D_MODEL), 0.1),
        'ln2_b': nrm(ks[23], (DEPTH, D_MODEL), 0.01),
    }


def reference(x, c, ctx, c_ctx, w_mod, w_in, na_rpb, ssm_conv_w, ssm_conv_b, ssm_a_log,
              ssm_dt_bias, ssm_d, ssm_norm_w, w_br_na, w_br_fn, w_br_ssm, w_out, ln1_g, ln1_b,
              w_ffn_gate, w_ffn_up, w_ffn_down, ln2_g, ln2_b):
    alpha = (2.0 * DEPTH) ** 0.25
    rope = _axial_rope(x.shape[1])
    c_act = jax.nn.silu(c)
    cc_act = jax.nn.silu(c_ctx)
    for l in range(DEPTH):
        with_ctx = l < DEPTH - 1
        sh1, sc1, g1, sh2, sc2, g2 = [m[:, None, :] for m in jnp.split(c_act @ w_mod[l], 6, axis=-1)]
        csh1, csc1, cg1, csh2, csc2, cg2 = jnp.split(cc_act @ w_mod[l], 6, axis=-1)
        mix_x, mix_c = _mixers(_modulate(x, sh1, sc1), _modulate(ctx, csh1, csc1), rope,
                               w_in[l], na_rpb[l], ssm_conv_w[l], ssm_conv_b[l], ssm_a_log[l],
                               ssm_dt_bias[l], ssm_d[l], ssm_norm_w[l], w_br_na[l], w_br_fn[l],
                               w_br_ssm[l], w_out[l], with_ctx)
        x = _layer_norm(alpha * x + g1 * mix_x, ln1_g[l], ln1_b[l])
        x = _layer_norm(alpha * x + g2 * _swiglu(_modulate(x, sh2, sc2), w_ffn_gate[l], w_ffn_up[l],
                                                 w_ffn_down[l]), ln2_g[l], ln2_b[l])
        if with_ctx:
            ctx = _layer_norm(alpha * ctx + cg1 * mix_c, ln1_g[l], ln1_b[l])
            ctx = _layer_norm(alpha * ctx + cg2 * _swiglu(_modulate(ctx, csh2, csc2), w_ffn_gate[l],
                                                          w_ffn_up[l], w_ffn_down[l]), ln2_g[l], ln2_b[l])
    return x
```

```python
import numpy as np
import concourse.bass as bass
import concourse.mybir as mybir
from concourse.bass_utils import run_bass_kernel_spmd

F32 = mybir.dt.float32
BF16 = mybir.dt.bfloat16
AF = mybir.ActivationFunctionType
ALU = mybir.AluOpType
AX = mybir.AxisListType

D = 2048
DEPTH = 4
SEQ = 2048
CTX = 256
TB = SEQ + CTX
BPC = 2
NCORES = 8
GRID_W = 64
NHEAD = 8
HD = 128
FFN = 5632
PROJ = 15424
ALPHA = (2.0 * DEPTH) ** 0.25
C_Q, C_K, C_V, C_U, C_XBC, C_Z, C_DT, C_G = 0, 1024, 2048, 3072, 4096, 7168, 9216, 9280


class Res:
    __slots__ = ("w", "r", "name")

    def __init__(self, name=""):
        self.w = None
        self.r = []
        self.name = name


class Sch:
    ENG = ("pe", "dve", "act", "pool", "sp")

    def __init__(self, nc):
        self.nc = nc
        self.eng = {"pe": nc.tensor, "dve": nc.vector, "act": nc.scalar, "pool": nc.gpsimd, "sp": nc.sync}
        self.sem = {}
        self.cnt = {}
        for e in self.ENG:
            self.sem[e] = nc.alloc_semaphore("sem_" + e)
            self.cnt[e] = 0
        self.known = {e: {} for e in self.ENG}
        self.dsem = {}
        self.dcnt = {}
        self.NSLOT = 48
        for i in range(self.NSLOT):
            k = "d%d" % i
            self.dsem[k] = nc.alloc_semaphore("dsem_" + k)
            self.dcnt[k] = 0
        self.slot_of = {}
        self.pending_dma = set()
        self.ninst = 0

    def _semh(self, key):
        return self.sem[key] if key in self.sem else self.dsem[key]

    def _wait(self, e, need):
        for key, v in need.items():
            if key in self.dsem:
                v = self.dcnt[key]
            if key == e and e == "pe":
                continue
            if self.known[e].get(key, 0) >= v:
                continue
            self.eng[e].wait_ge(self._semh(key), v)
            self.known[e][key] = v

    def _deps(self, reads, writes):
        need = {}
        for r in reads:
            if r.w is not None:
                k, v = r.w
                if need.get(k, 0) < v:
                    need[k] = v
        for w in writes:
            if w.w is not None:
                k, v = w.w
                if need.get(k, 0) < v:
                    need[k] = v
            for (k, v) in w.r:
                if need.get(k, 0) < v:
                    need[k] = v
        return need

    def _commit(self, ev, reads, writes):
        for r in reads:
            r.r = [x for x in r.r if x[0] != ev[0]] + [ev]
        for w in writes:
            w.w = ev
            w.r = []

    def op(self, e, fn, reads=(), writes=()):
        need = self._deps(reads, writes)
        self._wait(e, need)
        inst = fn()
        self.cnt[e] += 1
        inst.then_inc(self.sem[e], 1)
        self._commit((e, self.cnt[e]), reads, writes)
        self.ninst += 1
        return inst

    def dma(self, q, tkey, out, in_, reads=(), writes=()):
        if tkey not in self.slot_of:
            assert len(self.slot_of) < self.NSLOT, "out of DMA semaphore slots"
            self.slot_of[tkey] = "d%d" % len(self.slot_of)
        key = self.slot_of[tkey]
        need = self._deps(reads, writes)
        self._wait(q, need)
        inst = self.eng[q].dma_start(out=out, in_=in_)
        self.dcnt[key] += 16
        inst.then_inc(self.dsem[key], 16)
        ev = (key, self.dcnt[key])
        self._commit(ev, reads, writes)
        self.pending_dma.add(key)
        self.ninst += 1
        return inst

    def barrier(self):
        for key in sorted(self.pending_dma, key=str):
            if self.known["sp"].get(key, 0) < self.dcnt[key]:
                self.eng["sp"].wait_ge(self.dsem[key], self.dcnt[key])
                self.known["sp"][key] = self.dcnt[key]
        self.pending_dma = set()
        self.slot_of = {}
        for e in ("pe", "dve", "act", "pool"):
            if self.cnt[e] and self.known["sp"].get(e, 0) < self.cnt[e]:
                self.eng["sp"].wait_ge(self.sem[e], self.cnt[e])
                self.known["sp"][e] = self.cnt[e]
        self.cnt["sp"] += 1
        self.eng["sp"].nop().then_inc(self.sem["sp"], 1)
        for e in ("pe", "dve", "act", "pool"):
            self.eng[e].wait_ge(self.sem["sp"], self.cnt["sp"])
            self.known[e] = dict(self.known["sp"])
            self.known[e]["sp"] = self.cnt["sp"]


class Tl:
    __slots__ = ("a", "r", "name")

    def __init__(self, a, name):
        self.a = a
        self.r = Res(name)
        self.name = name


class Rot:
    def __init__(self, tiles):
        self.t = tiles
        self.i = 0

    def next(self):
        t = self.t[self.i % len(self.t)]
        self.i += 1
        return t


def interleave(gens):
    gens = list(gens)
    while gens:
        for g in list(gens):
            try:
                next(g)
            except StopIteration:
                gens.remove(g)


def build(nlayers=DEPTH, stop_after=None, dbg_names=(), first_layer_idx=0):
    from contextlib import ExitStack
    nc = bass.Bass("TRN2", target_bir_lowering=False)
    S = Sch(nc)
    L = nlayers

    def din(name, shape, dt=F32):
        return nc.dram_tensor(name, list(shape), dt, kind="ExternalInput").ap()

    def dscr(name, shape, dt=F32):
        kind = "ExternalOutput" if name in dbg_names else "Internal"
        return nc.dram_tensor(name, list(shape), dt, kind=kind).ap()

    x_in = din("x", [BPC, SEQ, D])
    ctx_in = din("ctx", [BPC, CTX, D])
    cT_in = din("cT", [128, 16, 3])
    ident_in = din("ident", [128, 128])
    perm_in = din("permrope", [128, 128])
    ropec_in = din("ropecos", [128, SEQ])
    ropes_in = din("ropesin", [128, SEQ])
    masks_in = din("masks", [128, 7, 128])
    ccsc_in = din("ccsc", [128, 2, 512], BF16)
    c256_in = din("c256", [128, 2, 512], BF16)
    dft_in = din("dft", [4, 128, 16, 1024], BF16)
    nabias_in = din("nabias", [L, NHEAD, 128, 5, 576])
    cw_in = din("convw", [L, 128, 24, 5])
    cb_in = din("convb", [L, 128, 24])
    vecs_in = din("vecs", [L, 8, D])
    w_mod = din("w_mod", [L, D, 6 * D])
    w_in = din("w_in", [L, D, PROJ])
    w_br = din("w_br", [L, 4096, D])
    w_out = din("w_out", [L, D, D])
    w_g = din("w_ffn_gate", [L, D, FFN])
    w_u = din("w_ffn_up", [L, D, FFN])
    w_d = din("w_ffn_down", [L, FFN, D])
    out_d = nc.dram_tensor("out", [BPC, SEQ, D], F32, kind="ExternalOutput").ap()

    QT = dscr("QT", [BPC, 1024, TB], BF16)
    KT = dscr("KT", [BPC, 1024, TB], BF16)
    VV = dscr("VV", [BPC, TB, 1024], BF16)
    UT = dscr("UT", [BPC, 1024, TB], BF16)
    XBCT = dscr("XBCT", [BPC, 3072, TB], F32)
    ZZ = dscr("ZZ", [BPC, TB, 2048], F32)
    DTT = dscr("DTT", [BPC, TB, 64], F32)
    GT = dscr("GT", [BPC, 6144, TB], F32)
    MODR = dscr("MODR", [3, 2 * D], F32)
    XS = dscr("XS", [BPC, SEQ, D], F32)
    X1 = dscr("X1", [BPC, SEQ, D], F32)
    CS = dscr("CS", [BPC, CTX, D], F32)
    C1 = dscr("C1", [BPC, CTX, D], F32)
    OT = dscr("OT", [BPC, 4096, TB], BF16)
    XSTM = dscr("XSTM", [BPC, TB, 2048], BF16)
    BTM = dscr("BTM", [BPC, TB, 512], BF16)
    BCT = dscr("BCT", [BPC, 1024, TB], BF16)
    YS = dscr("YS", [BPC, TB, 2048], F32)
    MT = dscr("MT", [BPC, 2048, TB], BF16)
    WGU = dscr("WGU", [2, 22, 128, 16 * 256], BF16)
    WD = dscr("WD", [16, 128, 22 * 256], BF16)

    es0 = ExitStack()

    uniq = [0]

    def sbt(es, name, shape, dt):
        uniq[0] += 1
        return Tl(es.enter_context(nc.sbuf_tensor("s%d_%s" % (uniq[0], name), list(shape), dt)), name)

    def pst(es, name, shape, dt=F32):
        uniq[0] += 1
        return Tl(es.enter_context(nc.psum_tensor("p%d_%s" % (uniq[0], name), list(shape), dt)), name)

    def dma(q, t, out, in_, reads=(), writes=()):
        S.dma(q, t.name, out, in_, reads=[x.r for x in reads], writes=[x.r for x in writes])

    def load(t, out, in_):
        dma("sp", t, out, in_, writes=[t])

    def store(t, out, in_):
        dma("act", t, out, in_, reads=[t])

    def op(e, fn, reads=(), writes=()):
        S.op(e, fn, [x.r for x in reads], [x.r for x in writes])

    def V(fn, reads, writes):
        op("dve", fn, reads, writes)

    def A(fn, reads, writes):
        op("act", fn, reads, writes)

    def P(fn, reads, writes):
        op("pe", fn, reads, writes)

    def G(fn, reads, writes):
        op("pool", fn, reads, writes)

    ev_i = [0]

    def evac(out, in_, reads, writes, scale=None):
        ev_i[0] += 1
        if ev_i[0] % 2:
            if scale is None:
                V(lambda: nc.vector.tensor_copy(out=out, in_=in_), reads, writes)
            else:
                V(lambda: nc.vector.tensor_scalar(out=out, in0=in_, scalar1=float(scale), scalar2=None, op0=ALU.mult), reads, writes)
        else:
            if scale is None:
                A(lambda: nc.scalar.copy(out=out, in_=in_), reads, writes)
            else:
                A(lambda: nc.scalar.activation(out=out, in_=in_, func=AF.Copy, scale=float(scale)), reads, writes)

    def mm(out, lhsT, rhs, start, stop):
        return nc.tensor.matmul(out, lhsT, rhs, start=start, stop=stop)

    eps_t = sbt(es0, "eps_t", [128, 2], F32)
    ident_f = sbt(es0, "ident_f", [128, 128], F32)
    ident_b = sbt(es0, "ident_b", [128, 128], BF16)
    cact = sbt(es0, "cact", [128, 16, 3], F32)
    modT = sbt(es0, "modT", [128, 64, 3], F32)
    stats_r = Rot([sbt(es0, "stats%d" % i, [128, 4, 6], F32) for i in range(4)])
    mv_r = Rot([sbt(es0, "mv%d" % i, [128, 8], F32) for i in range(4)])

    V(lambda: nc.vector.memset(eps_t.a[:, 0:1], 1e-6), [], [eps_t])
    V(lambda: nc.vector.memset(eps_t.a[:, 1:2], 1e-5), [], [eps_t])
    load(ident_f, ident_f.a[:], ident_in)
    V(lambda: nc.vector.tensor_copy(out=ident_b.a[:], in_=ident_f.a[:]), [ident_f], [ident_b])
    load(cact, cact.a[:], cT_in)
    A(lambda: nc.scalar.activation(out=cact.a[:], in_=cact.a[:], func=AF.Silu), [cact], [cact])

    def ln_stats_g(src, srct, eps_col):
        stats = stats_r.next()
        mv = mv_r.next()
        for j in range(4):
            V(lambda j=j: nc.vector.bn_stats(out=stats.a[:, j, :], in_=src[:, j * 512:(j + 1) * 512]), [srct], [stats])
        yield
        V(lambda: nc.vector.bn_aggr(out=mv.a[:, 0:2], in_=stats.a[:].rearrange("p a b -> p (a b)")), [stats], [mv])
        yield
        A(lambda: nc.scalar.activation(out=mv.a[:, 2:3], in_=mv.a[:, 1:2], func=AF.Sqrt, bias=eps_t.a[:, eps_col:eps_col + 1], scale=1.0),
          [mv, eps_t], [mv])
        yield
        V(lambda: nc.vector.reciprocal(out=mv.a[:, 3:4], in_=mv.a[:, 2:3]), [mv], [mv])
        yield
        V(lambda: nc.vector.tensor_scalar(out=mv.a[:, 4:5], in0=mv.a[:, 0:1], scalar1=mv.a[:, 3:4], scalar2=-1.0,
                                          op0=ALU.mult, op1=ALU.mult), [mv], [mv])
        yield
        return mv

    class WStream:
        def __init__(self, es, nk, ncols, tag):
            self.nk, self.ncols = nk, ncols
            self.st = Rot([sbt(es, "wst%s%d" % (tag, i), [128, nk, ncols], F32) for i in range(2)])
            self.bf = Rot([sbt(es, "wbf%s%d" % (tag, i), [128, nk, ncols], BF16) for i in range(2)])
            self.q = []

        def issue(self, wap, r0, nk, c0, ncols):
            t = self.st.next()
            src = wap[r0:r0 + nk * 128, c0:c0 + ncols].rearrange("(k p) n -> p k n", p=128)
            load(t, t.a[:, 0:nk, 0:ncols], src)
            self.q.append((t, nk, ncols))

        def get(self, cast=True):
            t, nk, ncols = self.q.pop(0)
            if not cast:
                return t
            b = self.bf.next()
            G(lambda: nc.gpsimd.tensor_copy(out=b.a[:, 0:nk, 0:ncols], in_=t.a[:, 0:nk, 0:ncols]), [t], [b])
            return b

    def run_groups(ws, groups, body, cast=True):
        if not groups:
            return
        g0 = groups[0]
        ws.issue(*g0[:5])
        for i, g in enumerate(groups):
            if i + 1 < len(groups):
                ws.issue(*groups[i + 1][:5])
            wt = ws.get(cast)
            body(wt, g)

    for li in range(L):
        l = li
        lg = first_layer_idx + li
        last = (lg == DEPTH - 1)
        xcur = x_in if li == 0 else XS
        ccur = ctx_in if li == 0 else CS
        xout = out_d if li == L - 1 else XS
        with ExitStack() as es:
            ws = WStream(es, 16, 512, "m")
            pb = Rot([pst(es, "pbm%d" % i, [128, 512]) for i in range(4)])
            stg = Rot([sbt(es, "stgm%d" % i, [128, 512], F32) for i in range(2)])
            groups = []
            for which, slot in [(0, 0), (1, 1), (3, 2), (4, 3)]:
                for g in range(4):
                    groups.append((w_mod[l], 0, 16, which * D + g * 512, 512, ("fm", slot * 16 + g * 4)))
            for gi, which in enumerate((2, 5)):
                for g in range(4):
                    groups.append((w_mod[l], 0, 16, which * D + g * 512, 512, ("row", gi * D + g * 512)))

            def body(wt, g):
                kind, dst = g[5]
                p = pb.next()
                if kind == "fm":
                    for m in range(4):
                        for k in range(16):
                            P(lambda m=m, k=k: mm(p.a[:, m * 4:m * 4 + 3], wt.a[:, k, m * 128:(m + 1) * 128], cact.a[:, k, :], k == 0, k == 15),
                              [wt, cact], [p])
                    V(lambda: nc.vector.tensor_copy(out=modT.a[:, dst:dst + 4, :],
                                                    in_=p.a[:, 0:16].rearrange("p (m r) -> p m r", r=4)[:, :, 0:3]), [p], [modT])
                else:
                    for k in range(16):
                        P(lambda k=k: mm(p.a[0:3, :], cact.a[:, k, :], wt.a[:, k, :], k == 0, k == 15), [wt, cact], [p])
                    s_ = stg.next()
                    V(lambda: nc.vector.tensor_copy(out=s_.a[0:3, :], in_=p.a[0:3, :]), [p], [s_])
                    store(s_, MODR[:, dst:dst + 512], s_.a[0:3, :])
            run_groups(ws, groups, body, cast=False)
            for slot in (1, 3):
                V(lambda slot=slot: nc.vector.tensor_scalar(out=modT.a[:, slot * 16:(slot + 1) * 16, :], in0=modT.a[:, slot * 16:(slot + 1) * 16, :],
                                                            scalar1=1.0, scalar2=None, op0=ALU.add), [modT], [modT])
        S.barrier()

        def modulate_g(es_bufs, src, srct, hT, tcol, row, sh_slot, sc_slot):
            xn, ptp = es_bufs
            mv = yield from ln_stats_g(src, srct, 0)
            xnt = xn.next()
            A(lambda: nc.scalar.activation(out=xnt.a[:], in_=src, func=AF.Identity, bias=mv.a[:, 4:5], scale=mv.a[:, 3:4]), [srct, mv], [xnt])
            yield
            for q4 in range(4):
                p = ptp.next()
                for c4 in range(4):
                    c = q4 * 4 + c4
                    P(lambda c=c, c4=c4: nc.tensor.transpose(p.a[:, c4 * 128:(c4 + 1) * 128], xnt.a[:, c * 128:(c + 1) * 128], ident_f.a[:]),
                      [xnt, ident_f], [p])
                for c4 in range(4):
                    c = q4 * 4 + c4
                    V(lambda c=c, c4=c4: nc.vector.tensor_scalar(
                        out=hT.a[:, c, tcol:tcol + 128], in0=p.a[:, c4 * 128:(c4 + 1) * 128],
                        scalar1=modT.a[:, sc_slot * 16 + c, row:row + 1], scalar2=modT.a[:, sh_slot * 16 + c, row:row + 1],
                        op0=ALU.mult, op1=ALU.add), [p, modT], [hT])
                yield

        with ExitStack() as es:
            ws = WStream(es, 16, 512, "a")
            hT = sbt(es, "hT", [128, 16, 1024], BF16)
            xt = Rot([sbt(es, "xt%d" % i, [128, D], F32) for i in range(3)])
            xn = Rot([sbt(es, "xn%d" % i, [128, D], F32) for i in range(3)])
            ptp = Rot([pst(es, "ptp%d" % i, [128, 512]) for i in range(2)])
            pb = Rot([pst(es, "pba%d" % i, [128, 512]) for i in range(6)])
            stg = Rot([sbt(es, "stga%d" % i, [128, 512], F32) for i in range(4)])
            stgb = Rot([sbt(es, "stgab%d" % i, [128, 512], BF16) for i in range(4)])
            blocks = [("ctx", 0, 0)] + [("lat", b, h) for b in range(BPC) for h in range(2)]
            for (bk, bb, bh) in blocks:
                ntile = 4 if bk == "ctx" else 8
                NT = ntile * 128

                def tile_src(t):
                    if bk == "ctx":
                        return ccur[t // 2, (t % 2) * 128:(t % 2 + 1) * 128, :]
                    return xcur[bb, bh * 1024 + t * 128: bh * 1024 + (t + 1) * 128, :]
                row = 2 if bk == "ctx" else bb
                def a1_tile(t):
                    tl = xt.next()
                    load(tl, tl.a[:], tile_src(t))
                    yield
                    yield from modulate_g((xn, ptp), tl.a[:], tl, hT, t * 128, row, 0, 1)
                for t0_ in range(0, ntile, 3):
                    interleave([a1_tile(t) for t in range(t0_, min(t0_ + 3, ntile))])

                def dst_fm(T_, f0):
                    if bk == "ctx":
                        return [(T_[0, f0:f0 + 128, 0:256], 0, 256), (T_[1, f0:f0 + 128, 0:256], 256, 256)]
                    t0 = CTX + bh * 1024
                    return [(T_[bb, f0:f0 + 128, t0:t0 + 1024], 0, 1024)]
                need_all = not (last and bk == "ctx")
                fm_list = []
                if need_all:
                    fm_list += [(QT, C_Q, 1024, BF16)]
                fm_list += [(KT, C_K, 1024, BF16)]
                if need_all:
                    fm_list += [(UT, C_U, 1024, BF16)]
                fm_list += [(XBCT, C_XBC, 3072, F32)]
                if need_all:
                    fm_list += [(GT, C_G, 6144, F32)]
                groups = []
                for (T_, c0, n, dt_) in fm_list:
                    for g in range(n // 512):
                        groups.append((w_in[l], 0, 16, c0 + g * 512, 512, ("fm", T_, g * 512, dt_)))
                tm_list = [(VV, C_V, 1024, BF16)]
                if need_all:
                    tm_list += [(ZZ, C_Z, 2048, F32)]
                for (T_, c0, n, dt_) in tm_list:
                    for g in range(n // 512):
                        groups.append((w_in[l], 0, 16, c0 + g * 512, 512, ("tm", T_, g * 512, dt_)))
                groups.append((w_in[l], 0, 16, C_DT, 64, ("tm", DTT, 0, F32)))

                def body(wt, g):
                    ncols = g[4]
                    kind, T_, d0, dt_ = g[5]
                    if kind == "fm":
                        for m in range(4):
                            dsts = dst_fm(T_, d0 + m * 128)
                            for n in range(NT // 512):
                                p = pb.next()
                                for k in range(16):
                                    P(lambda m=m, k=k, n=n: mm(p.a[:, :], wt.a[:, k, m * 128:(m + 1) * 128], hT.a[:, k, n * 512:(n + 1) * 512], k == 0, k == 15),
                                      [wt, hT], [p])
                                s_ = stgb.next() if dt_ == BF16 else stg.next()
                                evac(s_.a[:, :], p.a[:, :], [p], [s_])
                                for (dap, cc0, cn) in dsts:
                                    lo = max(cc0, n * 512)
                                    hi = min(cc0 + cn, (n + 1) * 512)
                                    if lo < hi:
                                        store(s_, dap[:, lo - cc0:hi - cc0], s_.a[:, lo - n * 512:hi - n * 512])
                    else:
                        for t in range(ntile):
                            p = pb.next()
                            for k in range(16):
                                P(lambda t=t, k=k: mm(p.a[:, 0:ncols], hT.a[:, k, t * 128:(t + 1) * 128], wt.a[:, k, 0:ncols], k == 0, k == 15),
                                  [wt, hT], [p])
                            s_ = stgb.next() if dt_ == BF16 else stg.next()
                            evac(s_.a[:, 0:ncols], p.a[:, 0:ncols], [p], [s_])
                            if bk == "ctx":
                                dap = T_[t // 2, (t % 2) * 128:(t % 2 + 1) * 128, d0:d0 + ncols]
                            else:
                                tk = CTX + bh * 1024 + t * 128
                                dap = T_[bb, tk:tk + 128, d0:d0 + ncols]
                            store(s_, dap, s_.a[:, 0:ncols])
                run_groups(ws, groups, body)
        S.barrier()
        if stop_after == "A":
            break
        with ExitStack() as es:
            permf = sbt(es, "permf", [128, 128], F32)
            permb = sbt(es, "permb", [128, 128], BF16)
            rc = sbt(es, "ropec", [128, SEQ], F32)
            rs = sbt(es, "ropes", [128, SEQ], F32)
            load(permf, permf.a[:], perm_in)
            V(lambda: nc.vector.tensor_copy(out=permb.a[:], in_=permf.a[:]), [permf], [permb])
            load(rc, rc.a[:], ropec_in)
            load(rs, rs.a[:], ropes_in)
            bias = Rot([sbt(es, "nab%d" % i, [128, 5, 576], F32) for i in range(2)])
            qT = Rot([sbt(es, "qT%d" % i, [128, TB], BF16) for i in range(2)])
            kT = Rot([sbt(es, "kT%d" % i, [128, TB], BF16) for i in range(2)])
            vt = Rot([sbt(es, "vt%d" % i, [128, 18, 128], BF16) for i in range(2)])
            qr = sbt(es, "qrot", [128, SEQ], BF16)
            kr = sbt(es, "krot", [128, SEQ], BF16)
            oT = Rot([sbt(es, "oT%d" % i, [128, TB], BF16) for i in range(2)])
            tmp1 = Rot([sbt(es, "rtmp%d" % i, [128, 512], F32) for i in range(2)])
            tmp2 = Rot([sbt(es, "rtmq%d" % i, [128, 512], F32) for i in range(2)])
            sful = Rot([sbt(es, "sful%d" % i, [128, 832], F32) for i in range(2)])
            pexp = Rot([sbt(es, "pexp%d" % i, [128, 832], BF16) for i in range(2)])
            pnrm = Rot([sbt(es, "pnrm%d" % i, [128, 832], BF16) for i in range(2)])
            pTs = Rot([sbt(es, "pTs%d" % i, [128, 896], BF16) for i in range(2)])
            sm = Rot([sbt(es, "sm%d" % i, [128, 4], F32) for i in range(4)])
            psA = Rot([pst(es, "psA%d" % i, [128, 512]) for i in range(2)])
            psB = Rot([pst(es, "psB%d" % i, [128, 512]) for i in range(2)])
            psT = Rot([pst(es, "psT%d" % i, [128, 1024], BF16) for i in range(2)])
            psO = Rot([pst(es, "psO%d" % i, [128, 512]) for i in range(2)])
            SCALE = float(HD) ** -0.5

            def softmax_pv(sf, W, blocks, vtl, o_t, ocol):
                st = sm.next()
                V(lambda: nc.vector.tensor_reduce(out=st.a[:, 0:1], in_=sf.a[:, 0:W], axis=AX.X, op=ALU.max), [sf], [st])
                V(lambda: nc.vector.tensor_scalar(out=st.a[:, 1:2], in0=st.a[:, 0:1], scalar1=-1.0, scalar2=None, op0=ALU.mult), [st], [st])
                yield
                pe_ = pexp.next()
                A(lambda: nc.scalar.activation(out=pe_.a[:, 0:W], in_=sf.a[:, 0:W], func=AF.Exp, bias=st.a[:, 1:2], scale=1.0, accum_out=st.a[:, 2:3]),
                  [sf, st], [pe_, st])
                yield
                V(lambda: nc.vector.reciprocal(out=st.a[:, 3:4], in_=st.a[:, 2:3]), [st], [st])
                pn = pnrm.next()
                V(lambda: nc.vector.tensor_scalar(out=pn.a[:, 0:W], in0=pe_.a[:, 0:W], scalar1=st.a[:, 3:4], scalar2=None, op0=ALU.mult), [pe_, st], [pn])
                yield
                pt = psT.next()
                for bi, (c0, nk, vti) in enumerate(blocks):
                    P(lambda bi=bi, c0=c0, nk=nk: nc.tensor.transpose(pt.a[0:nk, bi * 128:(bi + 1) * 128], pn.a[:, c0:c0 + nk], ident_b.a[:]),
                      [pn, ident_b], [pt])
                yield
                ps_ = pTs.next()
                nfull = sum(1 for b_ in blocks if b_[1] == 128)
                evac(ps_.a[:, 0:nfull * 128], pt.a[:, 0:nfull * 128], [pt], [ps_])
                if nfull < len(blocks):
                    evac(ps_.a[0:64, nfull * 128:(nfull + 1) * 128], pt.a[0:64, nfull * 128:(nfull + 1) * 128], [pt], [ps_])
                yield
                po = psO.next()
                for bi, (c0, nk, vti) in enumerate(blocks):
                    P(lambda bi=bi, nk=nk, vti=vti: mm(po.a[:, 0:128], vtl.a[0:nk, vti, :], ps_.a[0:nk, bi * 128:(bi + 1) * 128], bi == 0, bi == len(blocks) - 1),
                      [vtl, ps_], [po])
                yield
                evac(o_t.a[:, ocol:ocol + 128], po.a[:, 0:128], [po], [o_t])

            for h in range(NHEAD):
                bt = bias.next()
                load(bt, bt.a[:], nabias_in[l, h])
                for b in range(BPC):
                    q_ = qT.next(); k_ = kT.next(); v_ = vt.next(); o_ = oT.next()
                    if not last:
                        load(q_, q_.a[:], QT[b, h * 128:(h + 1) * 128, :])
                    else:
                        load(q_, q_.a[:, CTX:], QT[b, h * 128:(h + 1) * 128, CTX:])
                    load(k_, k_.a[:], KT[b, h * 128:(h + 1) * 128, :])
                    load(v_, v_.a[:], VV[b, :, h * 128:(h + 1) * 128].rearrange("(t p) d -> p t d", p=128))
                    for (src, dst) in ((q_, qr), (k_, kr)):
                        for n in range(4):
                            pp = psA.next()
                            P(lambda n=n: mm(pp.a[:, :], permb.a[:], src.a[:, CTX + n * 512:CTX + (n + 1) * 512], True, True), [permb, src], [pp])
                            t1 = tmp1.next(); t2 = tmp2.next()
                            V(lambda n=n: nc.vector.tensor_tensor(out=t1.a[:], in0=src.a[:, CTX + n * 512:CTX + (n + 1) * 512], in1=rc.a[:, n * 512:(n + 1) * 512], op=ALU.mult),
                              [src, rc], [t1])
                            V(lambda n=n: nc.vector.tensor_tensor(out=t2.a[:], in0=pp.a[:, :], in1=rs.a[:, n * 512:(n + 1) * 512], op=ALU.mult), [pp, rs], [t2])
                            V(lambda n=n: nc.vector.tensor_tensor(out=dst.a[:, n * 512:(n + 1) * 512], in0=t1.a[:], in1=t2.a[:], op=ALU.add), [t1, t2], [dst])
                    def lat_tile(j, q_=q_, k_=k_, v_=v_, o_=o_, bt=bt):
                        if j < 2:
                            start, nrows, typ = 0, 9, j
                        elif j < 14:
                            start, nrows, typ = 2 * j - 4, 9, 2
                        else:
                            start, nrows, typ = 24, 8, j - 11
                        W = 768 + (64 if nrows == 9 else 0)
                        pa = psA.next(); pb_ = psB.next()
                        P(lambda: mm(pa.a[:, :], qr.a[:, j * 128:(j + 1) * 128], kr.a[:, start * 64:start * 64 + 512], True, True), [qr, kr], [pa])
                        P(lambda: mm(pb_.a[:, 0:256], q_.a[:, CTX + j * 128:CTX + (j + 1) * 128], k_.a[:, 0:CTX], True, True), [q_, k_], [pb_])
                        if nrows == 9:
                            P(lambda: mm(pb_.a[:, 256:320], qr.a[:, j * 128:(j + 1) * 128], kr.a[:, start * 64 + 512:start * 64 + 576], True, True),
                              [qr, kr], [pb_])
                        yield
                        sf = sful.next()
                        V(lambda: nc.vector.scalar_tensor_tensor(out=sf.a[:, 0:512], in0=pa.a[:, :], scalar=SCALE, in1=bt.a[:, typ, 0:512], op0=ALU.mult, op1=ALU.add),
                          [pa, bt], [sf])
                        A(lambda: nc.scalar.activation(out=sf.a[:, 512:768], in_=pb_.a[:, 0:256], func=AF.Copy, scale=SCALE), [pb_], [sf])
                        if nrows == 9:
                            V(lambda: nc.vector.scalar_tensor_tensor(out=sf.a[:, 768:832], in0=pb_.a[:, 256:320], scalar=SCALE, in1=bt.a[:, typ, 512:576], op0=ALU.mult, op1=ALU.add),
                              [pb_, bt], [sf])
                        yield
                        blocks = [(i * 128, 128, 2 + start // 2 + i) for i in range(4)] + [(512, 128, 0), (640, 128, 1)]
                        if nrows == 9:
                            blocks.append((768, 64, 2 + start // 2 + 4))
                        yield from softmax_pv(sf, W, blocks, v_, o_, CTX + j * 128)
                    for j in range(0, 16, 2):
                        interleave([lat_tile(j), lat_tile(j + 1)])
                    if not last:
                        def ctx_tile(j, q_=q_, k_=k_, v_=v_, o_=o_):
                            pb_ = psB.next()
                            P(lambda: mm(pb_.a[:, 0:256], q_.a[:, j * 128:(j + 1) * 128], k_.a[:, 0:CTX], True, True), [q_, k_], [pb_])
                            yield
                            sf = sful.next()
                            A(lambda: nc.scalar.activation(out=sf.a[:, 0:256], in_=pb_.a[:, 0:256], func=AF.Copy, scale=SCALE), [pb_], [sf])
                            yield
                            yield from softmax_pv(sf, 256, [(0, 128, 0), (128, 128, 1)], v_, o_, j * 128)
                        interleave([ctx_tile(0), ctx_tile(1)])
                        store(o_, OT[b, h * 128:(h + 1) * 128, :], o_.a[:])
                    else:
                        store(o_, OT[b, h * 128:(h + 1) * 128, CTX:], o_.a[:, CTX:])
        S.barrier()
        if stop_after == "B":
            break
        with ExitStack() as es:
            ccsc = sbt(es, "ccsc", [128, 2, 512], BF16)
            c256 = sbt(es, "c256", [128, 2, 512], BF16)
            load(ccsc, ccsc.a[:], ccsc_in)
            load(c256, c256.a[:], c256_in)
            AB = sbt(es, "AB", [128, 18, 4, 512], BF16)
            pb = Rot([pst(es, "pbc%d" % i, [128, 512]) for i in range(6)])
            stgb = Rot([sbt(es, "stgcb%d" % i, [128, 512], BF16) for i in range(4)])
            SC_LAT = float(SEQ * 256) ** -0.5
            SC_CTX = float(CTX * 256) ** -0.5
            for b in range(BPC):
                t0 = 0 if not last else 2
                with ExitStack() as es2:
                    uT = sbt(es2, "uT", [128, 8, TB], BF16)
                    lo = 0 if not last else CTX
                    load(uT, uT.a[:, :, lo:], UT[b, :, lo:].rearrange("(c p) t -> p c t", p=128))
                    for t in range(t0, 18):
                        for g in range(4):
                            p = pb.next()
                            for kk in range(2):
                                P(lambda t=t, g=g, kk=kk: mm(p.a[:, :], uT.a[:, g * 2 + kk, t * 128:(t + 1) * 128], ccsc.a[:, kk, :], kk == 0, kk == 1),
                                  [uT, ccsc], [p])
                            evac(AB.a[:, t, g, :], p.a[:, :], [p], [AB])
                S.barrier()
                with ExitStack() as es2:
                    dft = Rot([sbt(es2, "dftb%d" % i, [128, 16, 1024], BF16) for i in range(2)])
                    nxt = dft.next()
                    load(nxt, nxt.a[:], dft_in[0])
                    for sb_ in range(4):
                        cur = nxt
                        if sb_ + 1 < 4:
                            nxt = dft.next()
                            load(nxt, nxt.a[:], dft_in[sb_ + 1])
                        for g in range(4):
                            for a in range(2):
                                p = pb.next()
                                for st_ in range(16):
                                    P(lambda g=g, a=a, st_=st_: mm(p.a[:, :], AB.a[:, 2 + st_, g, a * 128:(a + 1) * 128], cur.a[:, st_, 0:512], st_ == 0, False),
                                      [AB, cur], [p])
                                    P(lambda g=g, a=a, st_=st_: mm(p.a[:, :], AB.a[:, 2 + st_, g, 256 + a * 128:256 + (a + 1) * 128], cur.a[:, st_, 512:1024], False, st_ == 15),
                                      [AB, cur], [p])
                                s_ = stgb.next()
                                evac(s_.a[:, :], p.a[:, :], [p], [s_], scale=SC_LAT)
                                f0 = 1024 + g * 256 + a * 128
                                store(s_, OT[b, f0:f0 + 128, CTX + sb_ * 512:CTX + (sb_ + 1) * 512], s_.a[:, :])
                    if not last:
                        for g in range(4):
                            for a in range(2):
                                p = pb.next()
                                for st_ in range(2):
                                    P(lambda g=g, a=a, st_=st_: mm(p.a[:, 0:256], AB.a[:, st_, g, a * 128:(a + 1) * 128], c256.a[:, st_, 0:256], st_ == 0, False),
                                      [AB, c256], [p])
                                    P(lambda g=g, a=a, st_=st_: mm(p.a[:, 0:256], AB.a[:, st_, g, 256 + a * 128:256 + (a + 1) * 128], c256.a[:, st_, 256:512], False, st_ == 1),
                                      [AB, c256], [p])
                                s_ = stgb.next()
                                evac(s_.a[:, 0:256], p.a[:, 0:256], [p], [s_], scale=SC_CTX)
                                f0 = 1024 + g * 256 + a * 128
                                store(s_, OT[b, f0:f0 + 128, 0:CTX], s_.a[:, 0:256])
                S.barrier()
        S.barrier()
        if stop_after == "C":
            break
        with ExitStack() as es:
            cw = sbt(es, "cw", [128, 24, 5], F32)
            cb = sbt(es, "cb", [128, 24], F32)
            load(cw, cw.a[:], cw_in[l])
            load(cb, cb.a[:], cb_in[l])
            xin = Rot([sbt(es, "cxin%d" % i, [128, TB], F32) for i in range(2)])
            acc = Rot([sbt(es, "cacc%d" % i, [128, TB], F32) for i in range(2)])
            xo = Rot([sbt(es, "cxo%d" % i, [128, TB], BF16) for i in range(2)])
            tmo = Rot([sbt(es, "ctmo%d" % i, [128, 18, 128], BF16) for i in range(2)])
            ptc = Rot([pst(es, "ptc%d" % i, [128, 1024], BF16) for i in range(4)])
            for b in range(BPC):
                for cc in range(24):
                    xi = xin.next()
                    load(xi, xi.a[:], XBCT[b, cc * 128:(cc + 1) * 128, :])
                    ac = acc.next()
                    eng = V
                    ne = nc.vector
                    eng(lambda: ne.tensor_scalar(out=ac.a[:], in0=xi.a[:], scalar1=cw.a[:, cc, 2:3], scalar2=None, op0=ALU.mult), [xi, cw], [ac])
                    for (s0, s1) in ((0, CTX), (CTX, TB)):
                        for j in (0, 1, 3, 4):
                            sh = j - 2
                            if sh < 0:
                                o_sl = (s0 - sh, s1); i_sl = (s0, s1 + sh)
                            else:
                                o_sl = (s0, s1 - sh); i_sl = (s0 + sh, s1)
                            eng(lambda j=j, o_sl=o_sl, i_sl=i_sl: ne.scalar_tensor_tensor(
                                out=ac.a[:, o_sl[0]:o_sl[1]], in0=xi.a[:, i_sl[0]:i_sl[1]], scalar=cw.a[:, cc, j:j + 1],
                                in1=ac.a[:, o_sl[0]:o_sl[1]], op0=ALU.mult, op1=ALU.add), [xi, cw, ac], [ac])
                    xo_ = xo.next()
                    A(lambda: nc.scalar.activation(out=xo_.a[:], in_=ac.a[:], func=AF.Silu, bias=cb.a[:, cc:cc + 1], scale=1.0), [ac, cb], [xo_])
                    if cc >= 16:
                        store(xo_, BCT[b, (cc - 16) * 128:(cc - 15) * 128, :], xo_.a[:])
                    if cc < 20:
                        tm = tmo.next()
                        for t8 in range(0, 18, 8):
                            nt = min(8, 18 - t8)
                            p = ptc.next()
                            for t in range(nt):
                                P(lambda t=t, t8=t8: nc.tensor.transpose(p.a[:, t * 128:(t + 1) * 128], xo_.a[:, (t8 + t) * 128:(t8 + t + 1) * 128], ident_b.a[:]),
                                  [xo_, ident_b], [p])
                            evac(tm.a[:, t8:t8 + nt, :], p.a[:, 0:nt * 128].rearrange("p (t c) -> p t c", c=128), [p], [tm])
                        if cc < 16:
                            dst = XSTM[b, :, cc * 128:(cc + 1) * 128]
                        else:
                            dst = BTM[b, :, (cc - 16) * 128:(cc - 15) * 128]
                        store(tm, dst.rearrange("(t p) c -> p t c", p=128), tm.a[:])
        S.barrier()
        if stop_after == "D1":
            break
        with ExitStack() as es:
            masks = sbt(es, "masks", [128, 7, 128], F32)
            load(masks, masks.a[:], masks_in)
            vb = sbt(es, "vb", [128, 160], F32)
            load(vb, vb.a[:], vecs_in[l, 5:6, 0:160].partition_broadcast(128))
            nwb = sbt(es, "nwb", [128, D], F32)
            load(nwb, nwb.a[:], vecs_in[l, 4:5, :].partition_broadcast(128))
            A(lambda: nc.scalar.activation(out=vb.a[:, 64:128], in_=vb.a[:, 64:128], func=AF.Exp), [vb], [vb])
            V(lambda: nc.vector.tensor_scalar(out=vb.a[:, 64:128], in0=vb.a[:, 64:128], scalar1=-1.0, scalar2=None, op0=ALU.mult), [vb], [vb])
            one_t = sbt(es, "one_t", [128, 1], F32)
            V(lambda: nc.vector.memset(one_t.a[:], 1.0), [], [one_t])
            def mkbufs(ci):
                B_ = {}
                sfx = "_%d" % ci
                B_["xs_r"] = Rot([sbt(es, "sxs%d%s" % (i, sfx), [128, D], BF16) for i in range(2)])
                B_["btm_r"] = Rot([sbt(es, "sbtm%d%s" % (i, sfx), [128, 512], BF16) for i in range(2)])
                B_["bct_r"] = Rot([sbt(es, "sbct%d%s" % (i, sfx), [128, 8, 128], BF16) for i in range(2)])
                B_["dtr_r"] = Rot([sbt(es, "sdtr%d%s" % (i, sfx), [128, 32], F32) for i in range(2)])
                B_["dts"] = sbt(es, "dts" + sfx, [128, 2, 32], F32)
                B_["e3"] = sbt(es, "e3" + sfx, [128, 96], F32)
                B_["xdt"] = sbt(es, "xdt" + sfx, [128, D], BF16)
                B_["xdd"] = sbt(es, "xdd" + sfx, [128, D], BF16)
                B_["hst"] = sbt(es, "hst" + sfx, [128, D], F32)
                B_["hb"] = sbt(es, "hb" + sfx, [128, D], BF16)
                B_["GM"] = sbt(es, "GM" + sfx, [128, 4, 128], F32)
                B_["lt8"] = Rot([sbt(es, "lt8%d%s" % (i, sfx), [128, 8, 128], F32) for i in range(1)])
                B_["ltmp"] = Rot([sbt(es, "ltmp%d%s" % (i, sfx), [128, 512], F32) for i in range(2)])
                B_["Wt"] = Rot([sbt(es, "Wt%d%s" % (i, sfx), [128, 4, 128], BF16) for i in range(2)])
                B_["ycur_r"] = Rot([sbt(es, "ycur%d%s" % (i, sfx), [128, D], F32) for i in range(1)])
                B_["yprev_r"] = Rot([sbt(es, "yprev%d%s" % (i, sfx), [128, D], F32) for i in range(1)])
                B_["z_r"] = Rot([sbt(es, "zt%d%s" % (i, sfx), [128, D], F32) for i in range(1)])
                B_["obf"] = sbt(es, "obf" + sfx, [128, D], BF16)
                B_["oTt"] = Rot([sbt(es, "oTt%d%s" % (i, sfx), [128, 16, 128], BF16) for i in range(1)])
                B_["ssq"] = sbt(es, "ssq" + sfx, [128, 8], F32)
                return B_
            pcs = pst(es, "pcs", [128, 512])
            pg = pst(es, "pg", [128, 512])
            pd = Rot([pst(es, "pd%d" % i, [128, 512]) for i in range(2)])
            pyd = pst(es, "pyd", [128, 512])
            pyo = pst(es, "pyo", [128, 512])
            pss = pst(es, "pss", [128, 512])
            pto = pst(es, "pto", [128, 1024], BF16)

            def bc_last(ap, n, m):
                return ap.unsqueeze(2).to_broadcast([128, n, m])

            def bc_mid(ap, m, n):
                return ap.unsqueeze(1).to_broadcast([128, m, n])

            def chain(b, B_):
                xs_r, btm_r, bct_r, dtr_r = B_["xs_r"], B_["btm_r"], B_["bct_r"], B_["dtr_r"]
                dts, e3, xdt, xdd, hst, hb, GM = B_["dts"], B_["e3"], B_["xdt"], B_["xdd"], B_["hst"], B_["hb"], B_["GM"]
                lt8, ltmp, Wt, ycur_r, yprev_r, z_r = B_["lt8"], B_["ltmp"], B_["Wt"], B_["ycur_r"], B_["yprev_r"], B_["z_r"]
                obf, oTt, ssq = B_["obf"], B_["oTt"], B_["ssq"]
                for dr in range(2):
                    V(lambda: nc.vector.memset(hst.a[:], 0.0), [], [hst])
                    V(lambda: nc.vector.memset(hb.a[:], 0.0), [], [hb])
                    order = list(range(18)) if dr == 0 else [1, 0] + list(range(17, 1, -1))
                    m_acc, m_dec, m_01 = (0, 1, 5) if dr == 0 else (2, 3, 6)
                    for c in order:
                        need_y = (c >= 2) or (not last)
                        tk = c * 128
                        xs_ = xs_r.next(); btm = btm_r.next(); bct = bct_r.next(); dtr = dtr_r.next()
                        load(xs_, xs_.a[:], XSTM[b, tk:tk + 128, :])
                        load(btm, btm.a[:], BTM[b, tk:tk + 128, :])
                        load(bct, bct.a[:], BCT[b, :, tk:tk + 128].rearrange("(g p) t -> p g t", p=128))
                        load(dtr, dtr.a[:], DTT[b, tk:tk + 128, dr * 32:(dr + 1) * 32])
                        yield
                        V(lambda: nc.vector.tensor_tensor(out=dts.a[:, 0, :], in0=dtr.a[:], in1=vb.a[:, dr * 32:(dr + 1) * 32], op=ALU.add), [dtr, vb], [dts])
                        A(lambda: nc.scalar.activation(out=dts.a[:, 0, :], in_=dts.a[:, 0, :], func=AF.Exp), [dts], [dts])
                        A(lambda: nc.scalar.activation(out=dts.a[:, 0, :], in_=dts.a[:, 0, :], func=AF.Ln, bias=one_t.a[:, 0:1], scale=1.0), [dts, one_t], [dts])
                        V(lambda: nc.vector.tensor_tensor(out=dts.a[:, 1, :], in0=dts.a[:, 0, :], in1=vb.a[:, 64 + dr * 32:64 + (dr + 1) * 32], op=ALU.mult), [dts, vb], [dts])
                        yield
                        for i_, mi in enumerate((m_acc, m_dec, 4)):
                            P(lambda i_=i_, mi=mi: mm(pcs.a[:, i_ * 32:(i_ + 1) * 32], masks.a[:, mi, :], dts.a[:, 1, :], True, True), [masks, dts], [pcs])
                        A(lambda: nc.scalar.activation(out=e3.a[:], in_=pcs.a[:, 0:96], func=AF.Exp), [pcs], [e3])
                        yield
                        V(lambda: nc.vector.tensor_tensor(out=xdt.a[:].rearrange("p (h d) -> p h d", d=64), in0=xs_.a[:].rearrange("p (h d) -> p h d", d=64),
                                                          in1=bc_last(dts.a[:, 0, :], 32, 64), op=ALU.mult), [xs_, dts], [xdt])
                        V(lambda: nc.vector.tensor_tensor(out=xdd.a[:].rearrange("p (h d) -> p h d", d=64), in0=xdt.a[:].rearrange("p (h d) -> p h d", d=64),
                                                          in1=bc_last(e3.a[:, 32:64], 32, 64), op=ALU.mult), [xdt, e3], [xdd])
                        yield
                        if need_y:
                            for g in range(4):
                                P(lambda g=g: mm(pg.a[:, g * 128:(g + 1) * 128], bct.a[:, g, :], bct.a[:, 4 + g, :], True, True), [bct], [pg])
                            V(lambda: nc.vector.tensor_tensor(out=GM.a[:], in0=pg.a[:, :].rearrange("p (g l) -> p g l", l=128),
                                                              in1=bc_mid(masks.a[:, m_01, :], 4, 128), op=ALU.mult), [pg, masks], [GM])
                            yield
                            yc = ycur_r.next()
                            for g in range(4):
                                l8 = lt8.next()
                                V(lambda g=g: nc.vector.tensor_tensor(out=l8.a[:], in0=bc_mid(masks.a[:, m_dec, :], 8, 128),
                                                                      in1=bc_last(dts.a[:, 1, g * 8:(g + 1) * 8], 8, 128), op=ALU.mult), [masks, dts], [l8])
                                yield
                                for hf in range(2):
                                    p_ = pd.next()
                                    for r in range(4):
                                        P(lambda r=r, hf=hf: mm(p_.a[:, r * 128:(r + 1) * 128], l8.a[:, hf * 4 + r, :], masks.a[:, m_acc, :], True, True), [l8, masks], [p_])
                                    lt_ = ltmp.next()
                                    A(lambda: nc.scalar.activation(out=lt_.a[:], in_=p_.a[:, :], func=AF.Exp), [p_], [lt_])
                                    w_ = Wt.next()
                                    V(lambda g=g: nc.vector.tensor_tensor(out=w_.a[:], in0=lt_.a[:].rearrange("p (r l) -> p r l", l=128),
                                                                          in1=bc_mid(GM.a[:, g, :], 4, 128), op=ALU.mult), [lt_, GM], [w_])
                                    for r in range(4):
                                        hd_ = g * 8 + hf * 4 + r
                                        P(lambda r=r, hd_=hd_, hf=hf: mm(pyd.a[:, (hf * 4 + r) * 64:(hf * 4 + r + 1) * 64], w_.a[:, r, :], xdt.a[:, hd_ * 64:(hd_ + 1) * 64], True, True),
                                          [w_, xdt], [pyd])
                                P(lambda g=g: mm(pyo.a[:, :], bct.a[:, 4 + g, :], hb.a[:, g * 512:(g + 1) * 512], True, True), [bct, hb], [pyo])
                                V(lambda g=g: nc.vector.tensor_tensor(out=yc.a[:, g * 512:(g + 1) * 512].rearrange("p (h d) -> p h d", d=64),
                                                                      in0=pyo.a[:, :].rearrange("p (h d) -> p h d", d=64),
                                                                      in1=bc_last(e3.a[:, g * 8:(g + 1) * 8], 8, 64), op=ALU.mult), [pyo, e3], [yc])
                                V(lambda g=g: nc.vector.tensor_tensor(out=yc.a[:, g * 512:(g + 1) * 512], in0=pyd.a[:, :], in1=yc.a[:, g * 512:(g + 1) * 512], op=ALU.add),
                                  [pyd, yc], [yc])
                                yield
                        for g in range(4):
                            P(lambda g=g: mm(pss.a[:, :], btm.a[:, g * 128:(g + 1) * 128], xdd.a[:, g * 512:(g + 1) * 512], True, True), [btm, xdd], [pss])
                            V(lambda g=g: nc.vector.tensor_tensor(out=hst.a[:, g * 512:(g + 1) * 512].rearrange("p (h d) -> p h d", d=64),
                                                                  in0=hst.a[:, g * 512:(g + 1) * 512].rearrange("p (h d) -> p h d", d=64),
                                                                  in1=bc_last(e3.a[:, 64 + g * 8:64 + (g + 1) * 8], 8, 64), op=ALU.mult), [hst, e3], [hst])
                            V(lambda g=g: nc.vector.tensor_tensor(out=hst.a[:, g * 512:(g + 1) * 512], in0=pss.a[:, :], in1=hst.a[:, g * 512:(g + 1) * 512], op=ALU.add),
                              [pss, hst], [hst])
                            yield
                        A(lambda: nc.scalar.copy(out=hb.a[:], in_=hst.a[:]), [hst], [hb])
                        yield
                        if not need_y:
                            continue
                        if dr == 0:
                            yp = yprev_r.next()
                            V(lambda: nc.vector.tensor_tensor(out=yp.a[:].rearrange("p (h d) -> p h d", d=64), in0=xs_.a[:].rearrange("p (h d) -> p h d", d=64),
                                                              in1=bc_last(vb.a[:, 128:160], 32, 64), op=ALU.mult), [xs_, vb], [yp])
                            V(lambda: nc.vector.tensor_tensor(out=yc.a[:], in0=yc.a[:], in1=yp.a[:], op=ALU.add), [yc, yp], [yc])
                            store(yc, YS[b, tk:tk + 128, :], yc.a[:])
                        else:
                            yp = yprev_r.next(); zt = z_r.next()
                            load(yp, yp.a[:], YS[b, tk:tk + 128, :])
                            load(zt, zt.a[:], ZZ[b, tk:tk + 128, :])
                            yield
                            V(lambda: nc.vector.tensor_tensor(out=yc.a[:], in0=yc.a[:], in1=yp.a[:], op=ALU.add), [yc, yp], [yc])
                            A(lambda: nc.scalar.activation(out=zt.a[:], in_=zt.a[:], func=AF.Silu), [zt], [zt])
                            V(lambda: nc.vector.tensor_tensor(out=yc.a[:], in0=yc.a[:], in1=zt.a[:], op=ALU.mult), [yc, zt], [yc])
                            yield
                            for g in range(4):
                                A(lambda g=g: nc.scalar.activation(out=yp.a[:, g * 512:(g + 1) * 512], in_=yc.a[:, g * 512:(g + 1) * 512], func=AF.Square,
                                                                   accum_out=ssq.a[:, g:g + 1]), [yc], [yp, ssq])
                            A(lambda: nc.scalar.activation(out=ssq.a[:, 4:8], in_=ssq.a[:, 0:4], func=AF.Sqrt, bias=eps_t.a[:, 1:2], scale=1.0 / 512.0), [ssq, eps_t], [ssq])
                            V(lambda: nc.vector.reciprocal(out=ssq.a[:, 4:8], in_=ssq.a[:, 4:8]), [ssq], [ssq])
                            yield
                            for g in range(4):
                                V(lambda g=g: nc.vector.scalar_tensor_tensor(out=obf.a[:, g * 512:(g + 1) * 512], in0=yc.a[:, g * 512:(g + 1) * 512], scalar=ssq.a[:, 4 + g:5 + g],
                                                                             in1=nwb.a[:, g * 512:(g + 1) * 512], op0=ALU.mult, op1=ALU.mult), [yc, ssq, nwb], [obf])
                            ot = oTt.next()
                            for h8 in range(2):
                                for c8 in range(8):
                                    cc = h8 * 8 + c8
                                    P(lambda cc=cc, c8=c8: nc.tensor.transpose(pto.a[:, c8 * 128:(c8 + 1) * 128], obf.a[:, cc * 128:(cc + 1) * 128], ident_b.a[:]),
                                      [obf, ident_b], [pto])
                                evac(ot.a[:, h8 * 8:(h8 + 1) * 8, :], pto.a[:, :].rearrange("p (c t) -> p c t", t=128), [pto], [ot])
                            store(ot, OT[b, 2048:4096, tk:tk + 128].rearrange("(c p) t -> p c t", p=128), ot.a[:])
            interleave([chain(b, mkbufs(b)) for b in range(BPC)])
        S.barrier()
        if stop_after == "D2":
            break
        with ExitStack() as es:
            ws = WStream(es, 32, 128, "e")
            oTp = sbt(es, "oTp", [128, 32, 1024], BF16)
            gt_r = Rot([sbt(es, "gte%d" % i, [128, 1024], F32) for i in range(6)])
            acc_r = Rot([sbt(es, "acce%d" % i, [128, 1024], F32) for i in range(3)])
            tmp_r = Rot([sbt(es, "tmpe%d" % i, [128, 512], F32) for i in range(2)])
            mo_r = Rot([sbt(es, "moe%d" % i, [128, 1024], BF16) for i in range(2)])
            pb = Rot([pst(es, "pbe%d" % i, [128, 512]) for i in range(6)])
            blocks = [("lat", b, h) for b in range(BPC) for h in range(2)]
            if not last:
                blocks = [("ctx", 0, 0)] + blocks
            KR = [(0, 8), (8, 8), (16, 16)]
            for (bk, bb, bh) in blocks:
                NT = 512 if bk == "ctx" else 1024

                def segs():
                    if bk == "ctx":
                        return [(0, 0, 0, 256), (1, 0, 256, 256)]
                    return [(bb, CTX + bh * 1024, 0, 1024)]
                for (sb_, t0, c0, n) in segs():
                    load(oTp, oTp.a[:, :, c0:c0 + n], OT[sb_, :, t0:t0 + n].rearrange("(c p) t -> p c t", p=128))
                groups = [(w_br[l], 0, 32, fo * 128, 128, fo) for fo in range(16)]

                gate_q = {}

                def prep_gates(fo):
                    lst = []
                    for i in range(3):
                        gt = gt_r.next()
                        for (sb_, t0, c0, n) in segs():
                            load(gt, gt.a[:, c0:c0 + n], GT[sb_, i * 2048 + fo * 128:i * 2048 + (fo + 1) * 128, t0:t0 + n])
                        A(lambda gt=gt: nc.scalar.activation(out=gt.a[:, 0:NT], in_=gt.a[:, 0:NT], func=AF.Sigmoid), [gt], [gt])
                        lst.append(gt)
                    gate_q[fo] = lst
                prep_gates(0)

                def body(wt, g):
                    fo = g[5]
                    if fo + 1 < 16:
                        prep_gates(fo + 1)
                    gts = gate_q.pop(fo)
                    ac = acc_r.next()
                    for i in range(3):
                        gt = gts[i]
                        k0, nk = KR[i]
                        for n_ in range(NT // 512):
                            p = pb.next()
                            for k in range(nk):
                                P(lambda k=k, n_=n_: mm(p.a[:, :], wt.a[:, k0 + k, :], oTp.a[:, k0 + k, n_ * 512:(n_ + 1) * 512], k == 0, k == nk - 1), [wt, oTp], [p])
                            if i == 0:
                                V(lambda n_=n_: nc.vector.tensor_tensor(out=ac.a[:, n_ * 512:(n_ + 1) * 512], in0=p.a[:, :], in1=gt.a[:, n_ * 512:(n_ + 1) * 512], op=ALU.mult), [p, gt], [ac])
                            else:
                                tm = tmp_r.next()
                                V(lambda n_=n_: nc.vector.tensor_tensor(out=tm.a[:], in0=p.a[:, :], in1=gt.a[:, n_ * 512:(n_ + 1) * 512], op=ALU.mult), [p, gt], [tm])
                                V(lambda n_=n_: nc.vector.tensor_tensor(out=ac.a[:, n_ * 512:(n_ + 1) * 512], in0=ac.a[:, n_ * 512:(n_ + 1) * 512], in1=tm.a[:], op=ALU.add), [ac, tm], [ac])
                    mo = mo_r.next()
                    A(lambda: nc.scalar.copy(out=mo.a[:, 0:NT], in_=ac.a[:, 0:NT]), [ac], [mo])
                    for (sb_, t0, c0, n) in segs():
                        store(mo, MT[sb_, fo * 128:(fo + 1) * 128, t0:t0 + n], mo.a[:, c0:c0 + n])
                run_groups(ws, groups, body)
        S.barrier()
        if stop_after == "E":
            break

        def ln_affine_g(tl, gb, bb_, dst):
            mv = yield from ln_stats_g(tl.a[:], tl, 0)
            A(lambda: nc.scalar.activation(out=tl.a[:], in_=tl.a[:], func=AF.Identity, bias=mv.a[:, 4:5], scale=mv.a[:, 3:4]), [tl, mv], [tl])
            yield
            V(lambda: nc.vector.tensor_tensor(out=tl.a[:], in0=tl.a[:], in1=gb.a[:], op=ALU.mult), [tl, gb], [tl])
            yield
            G(lambda: nc.gpsimd.tensor_tensor(out=tl.a[:], in0=tl.a[:], in1=bb_.a[:], op=ALU.add), [tl, bb_], [tl])
            yield
            store(tl, dst, tl.a[:])

        with ExitStack() as es:
            ws = WStream(es, 16, 256, "f")
            mTp = sbt(es, "mTp", [128, 16, 1024], BF16)
            xb = [sbt(es, "xb%d" % i, [128, D], F32) for i in range(8)]
            g1b = sbt(es, "g1b", [128, D], F32)
            lng = sbt(es, "lng", [128, D], F32)
            lnb = sbt(es, "lnb", [128, D], F32)
            load(lng, lng.a[:], vecs_in[l, 0:1, :].partition_broadcast(128))
            load(lnb, lnb.a[:], vecs_in[l, 1:2, :].partition_broadcast(128))
            tmp_r = Rot([sbt(es, "tmpf%d" % i, [128, 256], F32) for i in range(2)])
            pb = Rot([pst(es, "pbf%d" % i, [128, 512]) for i in range(6)])
            blocks = [("lat", b, h) for b in range(BPC) for h in range(2)]
            if not last:
                blocks = [("ctx", 0, 0)] + blocks
            for (bk, bb, bh) in blocks:
                ntile = 4 if bk == "ctx" else 8
                row = 2 if bk == "ctx" else bb
                load(g1b, g1b.a[:], MODR[row:row + 1, 0:D].partition_broadcast(128))
                if bk == "ctx":
                    for sb_ in range(2):
                        load(mTp, mTp.a[:, :, sb_ * 256:(sb_ + 1) * 256], MT[sb_, :, 0:256].rearrange("(c p) t -> p c t", p=128))
                else:
                    t0 = CTX + bh * 1024
                    load(mTp, mTp.a[:, :, 0:1024], MT[bb, :, t0:t0 + 1024].rearrange("(c p) t -> p c t", p=128))

                def xsrc(T_c, T_x, t):
                    if bk == "ctx":
                        return T_c[t // 2, (t % 2) * 128:(t % 2 + 1) * 128, :]
                    return T_x[bb, bh * 1024 + t * 128: bh * 1024 + (t + 1) * 128, :]
                for t in range(ntile):
                    load(xb[t], xb[t].a[:], xsrc(ccur, xcur, t))
                groups = [(w_out[l], 0, 16, cg * 256, 256, cg) for cg in range(8)]

                def body(wt, g):
                    cg = g[5]
                    for t in range(ntile):
                        p = pb.next()
                        for k in range(16):
                            P(lambda t=t, k=k: mm(p.a[:, 0:256], mTp.a[:, k, t * 128:(t + 1) * 128], wt.a[:, k, :], k == 0, k == 15), [wt, mTp], [p])
                        tm = tmp_r.next()
                        V(lambda: nc.vector.tensor_tensor(out=tm.a[:], in0=p.a[:, 0:256], in1=g1b.a[:, cg * 256:(cg + 1) * 256], op=ALU.mult), [p, g1b], [tm])
                        V(lambda t=t: nc.vector.scalar_tensor_tensor(out=xb[t].a[:, cg * 256:(cg + 1) * 256], in0=xb[t].a[:, cg * 256:(cg + 1) * 256], scalar=float(ALPHA),
                                                                     in1=tm.a[:], op0=ALU.mult, op1=ALU.add), [xb[t], tm], [xb[t]])
                run_groups(ws, groups, body)
                for t0_ in range(0, ntile, 4):
                    interleave([ln_affine_g(xb[t], lng, lnb, xsrc(C1, X1, t)) for t in range(t0_, t0_ + 4)])
        S.barrier()
        if stop_after == "F1":
            break

        with ExitStack() as es:
            ws = WStream(es, 22, 256, "w")
            groups = []
            for gi in range(22):
                groups.append((w_g[l], 0, 16, gi * 256, 256, WGU[0, gi]))
                groups.append((w_u[l], 0, 16, gi * 256, 256, WGU[1, gi]))
            for cg in range(8):
                for hf in range(2):
                    groups.append((w_d[l], hf * 22 * 128, 22, cg * 256, 256, WD[cg * 2 + hf]))
            cast_i = [0]
            wcb = Rot([sbt(es, "wcb%d" % i, [128, 22 * 256], BF16) for i in range(3)])

            def body(wt, g):
                nk = g[2]
                b_ = wcb.next()
                n = nk * 256
                src = wt.a[:, 0:nk, :].rearrange("p k n -> p (k n)")
                cast_i[0] += 1
                e = cast_i[0] % 3
                if e == 0:
                    V(lambda: nc.vector.tensor_copy(out=b_.a[:, 0:n], in_=src), [wt], [b_])
                elif e == 1:
                    A(lambda: nc.scalar.copy(out=b_.a[:, 0:n], in_=src), [wt], [b_])
                else:
                    G(lambda: nc.gpsimd.tensor_copy(out=b_.a[:, 0:n], in_=src), [wt], [b_])
                store(b_, g[5][:, 0:n], b_.a[:, 0:n])
            run_groups(ws, groups, body, cast=False)
        S.barrier()

        with ExitStack() as es:
            wsb = Rot([sbt(es, "wsb%d" % i, [128, 22 * 256], BF16) for i in range(4)])
            h2T = sbt(es, "h2T", [128, 16, 512], BF16)
            aT = sbt(es, "aT", [128, 44, 512], BF16)
            xb = [sbt(es, "xg%d" % i, [128, D], F32) for i in range(4)]
            xn = Rot([sbt(es, "xng%d" % i, [128, D], F32) for i in range(4)])
            g2b = sbt(es, "g2b", [128, D], F32)
            lng = sbt(es, "lng2", [128, D], F32)
            lnb = sbt(es, "lnb2", [128, D], F32)
            load(lng, lng.a[:], vecs_in[l, 2:3, :].partition_broadcast(128))
            load(lnb, lnb.a[:], vecs_in[l, 3:4, :].partition_broadcast(128))
            tmp_r = Rot([sbt(es, "tmpg%d" % i, [128, 512], F32) for i in range(2)])
            pgu_all = [pst(es, "pgu%d" % i, [128, 512]) for i in range(4)]
            pdn_l = [pst(es, "pdn%d" % i, [128, 512]) for i in range(4)]
            pdn = Rot(pdn_l)
            ptp = Rot(pdn_l[0:2])
            blocks = [("lat", b, h) for b in range(BPC) for h in range(4)]
            if not last:
                blocks = [("ctx", 0, 0)] + blocks
            for (bk, bb, bh) in blocks:
                row = 2 if bk == "ctx" else bb
                load(g2b, g2b.a[:], MODR[row:row + 1, D:2 * D].partition_broadcast(128))

                def xsrc(T_c, T_x, t):
                    if bk == "ctx":
                        return T_c[t // 2, (t % 2) * 128:(t % 2 + 1) * 128, :]
                    return T_x[bb, bh * 512 + t * 128: bh * 512 + (t + 1) * 128, :]
                def f2_tile(t):
                    load(xb[t], xb[t].a[:], xsrc(C1, X1, t))
                    yield
                    yield from modulate_g((xn, ptp), xb[t].a[:], xb[t], h2T, t * 128, row, 2, 3)
                interleave([f2_tile(t) for t in range(4)])
                groups = []
                for gi in range(22):
                    groups.append((WGU[0, gi], 16 * 256, ("g", gi)))
                    groups.append((WGU[1, gi], 16 * 256, ("u", gi)))
                for cg in range(8):
                    for hf in range(2):
                        groups.append((WD[cg * 2 + hf], 22 * 256, ("d", cg, hf)))
                dn_banks = {}
                gt_hold = {}

                def issue(g):
                    t_ = wsb.next()
                    load(t_, t_.a[:, 0:g[1]], g[0][:, 0:g[1]])
                    return t_

                def body(wt, g):
                    pl = g[2]
                    if pl[0] == "g":
                        gt_hold[0] = wt
                        return
                    if pl[0] == "u":
                        wg_ = gt_hold[0]
                        for jj in range(2):
                            j = pl[1] * 2 + jj
                            pgu = pgu_all[0:2] if j % 2 == 0 else pgu_all[2:4]
                            for (w__, p) in ((wg_, pgu[0]), (wt, pgu[1])):
                                wv = w__.a[:, 0:4096].rearrange("p (k n) -> p k n", n=256)
                                for k in range(16):
                                    P(lambda k=k, wv=wv, p=p: mm(p.a[:, :], wv[:, k, jj * 128:(jj + 1) * 128], h2T.a[:, k, :], k == 0, k == 15), [w__, h2T], [p])
                            tm = tmp_r.next()
                            A(lambda: nc.scalar.activation(out=tm.a[:], in_=pgu[0].a[:, :], func=AF.Silu), [pgu[0]], [tm])
                            V(lambda j=j: nc.vector.tensor_tensor(out=aT.a[:, j, :], in0=pgu[1].a[:, :], in1=tm.a[:], op=ALU.mult), [pgu[1], tm], [aT])
                    else:
                        _, cg, hf = pl
                        wv = wt.a[:, 0:22 * 256].rearrange("p (k n) -> p k n", n=256)
                        if hf == 0:
                            dn_banks[cg] = [pdn.next() for _ in range(4)]
                        for t in range(4):
                            p = dn_banks[cg][t]
                            for k in range(22):
                                P(lambda t=t, k=k, p=p: mm(p.a[:, 0:256], aT.a[:, hf * 22 + k, t * 128:(t + 1) * 128], wv[:, k, :], hf == 0 and k == 0, hf == 1 and k == 21),
                                  [wt, aT], [p])
                            if hf == 1:
                                tm = tmp_r.next()
                                V(lambda p=p: nc.vector.tensor_tensor(out=tm.a[:, 0:256], in0=p.a[:, 0:256], in1=g2b.a[:, cg * 256:(cg + 1) * 256], op=ALU.mult), [p, g2b], [tm])
                                V(lambda t=t: nc.vector.scalar_tensor_tensor(out=xb[t].a[:, cg * 256:(cg + 1) * 256], in0=xb[t].a[:, cg * 256:(cg + 1) * 256], scalar=float(ALPHA),
                                                                             in1=tm.a[:, 0:256], op0=ALU.mult, op1=ALU.add), [xb[t], tm], [xb[t]])
                q = [issue(groups[0]), issue(groups[1])]
                for gi_, g in enumerate(groups):
                    if gi_ + 2 < len(groups):
                        q.append(issue(groups[gi_ + 2]))
                    body(q.pop(0), g)
                interleave([ln_affine_g(xb[t], lng, lnb, xsrc(CS, xout, t)) for t in range(4)])
        S.barrier()

    es0.close()
    return nc


_CONST_CACHE = {}


def _constants():
    if _CONST_CACHE:
        return _CONST_CACHE
    import ml_dtypes
    bf = ml_dtypes.bfloat16
    c = {}
    c["ident"] = np.eye(128, dtype=np.float32)
    d = np.arange(128)
    half = (d % 64) // 32
    partner = np.where(half == 0, d + 32, d - 32)
    perm = np.zeros((128, 128), np.float32)
    perm[partner, d] = 1.0
    c["permrope"] = perm
    t = np.arange(SEQ)
    row = (t // GRID_W).astype(np.float32)
    col = (t % GRID_W).astype(np.float32)
    inv = (np.float32(10000.0) ** (-np.arange(32, dtype=np.float32) / np.float32(32))).astype(np.float32)
    pos = np.where((d // 64)[:, None] == 0, row[None, :], col[None, :]).astype(np.float32)
    ang = (pos * inv[d % 32][:, None]).astype(np.float32)
    c["ropecos"] = np.cos(ang).astype(np.float32)
    c["ropesin"] = (np.sin(ang) * np.where(half == 0, -1.0, 1.0)[:, None]).astype(np.float32)
    k = np.arange(128)
    K_, L_ = np.meshgrid(k, k, indexing="ij")
    masks = np.stack([K_ <= L_, K_ > L_, K_ >= L_, K_ < L_, np.ones_like(K_, bool), L_ >= K_, L_ <= K_], axis=1)
    c["masks"] = np.ascontiguousarray(masks.astype(np.float32))
    ch = np.arange(256)
    a256 = 2.0 * np.pi * ((ch[:, None] * ch[None, :]) % 256) / 256.0
    C256, S256 = np.cos(a256), np.sin(a256)
    ccsc = np.concatenate([C256, S256], axis=1).reshape(2, 128, 512).transpose(1, 0, 2)
    c256 = np.concatenate([C256, -S256], axis=1).reshape(2, 128, 512).transpose(1, 0, 2)
    c["ccsc"] = np.ascontiguousarray(ccsc).astype(bf)
    c["c256"] = np.ascontiguousarray(c256).astype(bf)
    s = np.arange(SEQ)
    a2k = 2.0 * np.pi * ((s[:, None] * s[None, :]) % SEQ) / float(SEQ)
    Cs, Ss = np.cos(a2k).astype(np.float32), np.sin(a2k).astype(np.float32)
    dft = np.empty((4, 128, 16, 1024), np.float32)
    for sb_ in range(4):
        dft[sb_, :, :, 0:512] = Cs[:, sb_ * 512:(sb_ + 1) * 512].reshape(16, 128, 512).transpose(1, 0, 2)
        dft[sb_, :, :, 512:1024] = -Ss[:, sb_ * 512:(sb_ + 1) * 512].reshape(16, 128, 512).transpose(1, 0, 2)
    c["dft"] = dft.astype(bf)
    _CONST_CACHE.update(c)
    return _CONST_CACHE


def _nabias(rpb):
    L = rpb.shape[0]
    out = np.full((L, NHEAD, 128, 5, 576), -30000.0, np.float32)
    qc = np.arange(64)
    kc = np.arange(64)
    c_start = np.clip(qc - 8, 0, 48)
    in_col = (kc[None, :] >= c_start[:, None]) & (kc[None, :] < c_start[:, None] + 16)
    dc = np.clip(kc[None, :] - qc[:, None], -15, 15) + 15
    types = [(0, 0, 9), (1, 0, 9), (2, 0, 9), (14, 24, 8), (15, 24, 8)]
    for ty, (j, start, nrows) in enumerate(types):
        for qr in range(2):
            q_abs = 2 * j + qr
            rs = int(np.clip(q_abs - 4, 0, 24))
            for kr in range(nrows):
                k_abs = start + kr
                if not (rs <= k_abs < rs + 8):
                    continue
                dr = k_abs - q_abs + 7
                vals = rpb[:, :, dr, :][:, :, dc]
                vals = np.where(in_col[None, None], vals, np.float32(-30000.0))
                out[:, :, qr * 64:(qr + 1) * 64, ty, kr * 64:(kr + 1) * 64] = vals
    return out


def _prep_shared(inputs, nlayers=DEPTH, l0=0):
    f = lambda k: np.asarray(inputs[k], np.float32)[l0:l0 + nlayers]
    sh = dict(_constants())
    sh["nabias"] = _nabias(f("na_rpb"))
    cw = f("ssm_conv_w")
    sh["convw"] = np.ascontiguousarray(cw.reshape(nlayers, 5, 24, 128).transpose(0, 3, 2, 1))
    sh["convb"] = np.ascontiguousarray(f("ssm_conv_b").reshape(nlayers, 24, 128).transpose(0, 2, 1))
    vecs = np.zeros((nlayers, 8, D), np.float32)
    vecs[:, 0] = f("ln1_g"); vecs[:, 1] = f("ln1_b"); vecs[:, 2] = f("ln2_g"); vecs[:, 3] = f("ln2_b")
    vecs[:, 4] = f("ssm_norm_w")
    vecs[:, 5, 0:64] = f("ssm_dt_bias").reshape(nlayers, 64)
    vecs[:, 5, 64:128] = f("ssm_a_log").reshape(nlayers, 64)
    vecs[:, 5, 128:160] = f("ssm_d")
    sh["vecs"] = vecs
    sh["w_mod"] = np.ascontiguousarray(f("w_mod"))
    sh["w_in"] = np.ascontiguousarray(f("w_in"))
    sh["w_br"] = np.concatenate([f("w_br_na"), f("w_br_fn"), f("w_br_ssm")], axis=1)
    sh["w_out"] = np.ascontiguousarray(f("w_out"))
    sh["w_ffn_gate"] = np.ascontiguousarray(f("w_ffn_gate"))
    sh["w_ffn_up"] = np.ascontiguousarray(f("w_ffn_up"))
    sh["w_ffn_down"] = np.ascontiguousarray(f("w_ffn_down"))
    return sh


def _prep_core(inputs, core, shared):
    b0 = core * BPC
    c = np.asarray(inputs["c"], np.float32)
    c_ctx = np.asarray(inputs["c_ctx"], np.float32)
    cin = np.stack([c[b0], c[b0 + 1], c_ctx], axis=0)
    m = dict(shared)
    m["x"] = np.ascontiguousarray(np.asarray(inputs["x"], np.float32)[b0:b0 + BPC])
    m["ctx"] = np.ascontiguousarray(np.asarray(inputs["ctx"], np.float32)[b0:b0 + BPC])
    m["cT"] = np.ascontiguousarray(cin.T.reshape(16, 128, 3).transpose(1, 0, 2))
    return m


def kernel(**inputs):
    nc = build(DEPTH)
    shared = _prep_shared(inputs)
    in_maps = [_prep_core(inputs, core, shared) for core in range(NCORES)]
    res = run_bass_kernel_spmd(nc, in_maps, core_ids=list(range(NCORES)))
    out = np.concatenate([np.asarray(r["out"], np.float32) for r in res.results], axis=0)
    return out
```

```python
import numpy as np
import concourse.bass as bass
import concourse.mybir as mybir
from concourse.bass_utils import run_bass_kernel_spmd

F32 = mybir.dt.float32
BF16 = mybir.dt.bfloat16
AF = mybir.ActivationFunctionType
ALU = mybir.AluOpType
AX = mybir.AxisListType

D = 2048
DEPTH = 4
SEQ = 2048
CTX = 256
TB = SEQ + CTX
BPC = 2
NCORES = 8
GRID_W = 64
NHEAD = 8
HD = 128
FFN = 5632
PROJ = 15424
ALPHA = (2.0 * DEPTH) ** 0.25
C_Q, C_K, C_V, C_U, C_XBC, C_Z, C_DT, C_G = 0, 1024, 2048, 3072, 4096, 7168, 9216, 9280


class Res:
    __slots__ = ("w", "r", "name")

    def __init__(self, name=""):
        self.w = None
        self.r = []
        self.name = name


class Sch:
    ENG = ("pe", "dve", "act", "pool", "sp")

    def __init__(self, nc):
        self.nc = nc
        self.eng = {"pe": nc.tensor, "dve": nc.vector, "act": nc.scalar, "pool": nc.gpsimd, "sp": nc.sync}
        self.sem = {}
        self.cnt = {}
        for e in self.ENG:
            self.sem[e] = nc.alloc_semaphore("sem_" + e)
            self.cnt[e] = 0
        self.known = {e: {} for e in self.ENG}
        self.dsem = {}
        self.dcnt = {}
        self.NSLOT = 48
        for i in range(self.NSLOT):
            k = "d%d" % i
            self.dsem[k] = nc.alloc_semaphore("dsem_" + k)
            self.dcnt[k] = 0
        self.slot_of = {}
        self.pending_dma = set()
        self.ninst = 0

    def _semh(self, key):
        return self.sem[key] if key in self.sem else self.dsem[key]

    def _wait(self, e, need):
        for key, v in need.items():
            if key in self.dsem:
                v = self.dcnt[key]
            if key == e and e == "pe":
                continue
            if self.known[e].get(key, 0) >= v:
                continue
            self.eng[e].wait_ge(self._semh(key), v)
            self.known[e][key] = v

    def _deps(self, reads, writes):
        need = {}
        for r in reads:
            if r.w is not None:
                k, v = r.w
                if need.get(k, 0) < v:
                    need[k] = v
        for w in writes:
            if w.w is not None:
                k, v = w.w
                if need.get(k, 0) < v:
                    need[k] = v
            for (k, v) in w.r:
                if need.get(k, 0) < v:
                    need[k] = v
        return need

    def _commit(self, ev, reads, writes):
        for r in reads:
            r.r = [x for x in r.r if x[0] != ev[0]] + [ev]
        for w in writes:
            w.w = ev
            w.r = []

    def op(self, e, fn, reads=(), writes=()):
        need = self._deps(reads, writes)
        self._wait(e, need)
        inst = fn()
        self.cnt[e] += 1
        inst.then_inc(self.sem[e], 1)
        self._commit((e, self.cnt[e]), reads, writes)
        self.ninst += 1
        return inst

    def dma(self, q, tkey, out, in_, reads=(), writes=()):
        if tkey not in self.slot_of:
            assert len(self.slot_of) < self.NSLOT, "out of DMA semaphore slots"
            self.slot_of[tkey] = "d%d" % len(self.slot_of)
        key = self.slot_of[tkey]
        need = self._deps(reads, writes)
        self._wait(q, need)
        inst = self.eng[q].dma_start(out=out, in_=in_)
        self.dcnt[key] += 16
        inst.then_inc(self.dsem[key], 16)
        ev = (key, self.dcnt[key])
        self._commit(ev, reads, writes)
        self.pending_dma.add(key)
        self.ninst += 1
        return inst

    def barrier(self):
        for key in sorted(self.pending_dma, key=str):
            if self.known["sp"].get(key, 0) < self.dcnt[key]:
                self.eng["sp"].wait_ge(self.dsem[key], self.dcnt[key])
                self.known["sp"][key] = self.dcnt[key]
        self.pending_dma = set()
        self.slot_of = {}
        for e in ("pe", "dve", "act", "pool"):
            if self.cnt[e] and self.known["sp"].get(e, 0) < self.cnt[e]:
                self.eng["sp"].wait_ge(self.sem[e], self.cnt[e])
                self.known["sp"][e] = self.cnt[e]
        self.cnt["sp"] += 1
        self.eng["sp"].nop().then_inc(self.sem["sp"], 1)
        for e in ("pe", "dve", "act", "pool"):
            self.eng[e].wait_ge(self.sem["sp"], self.cnt["sp"])
            self.known[e] = dict(self.known["sp"])
            self.known[e]["sp"] = self.cnt["sp"]


class Tl:
    __slots__ = ("a", "r", "name")

    def __init__(self, a, name):
        self.a = a
        self.r = Res(name)
        self.name = name


class Rot:
    def __init__(self, tiles):
        self.t = tiles
        self.i = 0

    def next(self):
        t = self.t[self.i % len(self.t)]
        self.i += 1
        return t


def interleave(gens):
    gens = list(gens)
    while gens:
        for g in list(gens):
            try:
                next(g)
            except StopIteration:
                gens.remove(g)


def build(nlayers=DEPTH, stop_after=None, dbg_names=(), first_layer_idx=0):
    from contextlib import ExitStack
    nc = bass.Bass("TRN2", target_bir_lowering=False)
    S = Sch(nc)
    L = nlayers

    def din(name, shape, dt=F32):
        return nc.dram_tensor(name, list(shape), dt, kind="ExternalInput").ap()

    def dscr(name, shape, dt=F32):
        kind = "ExternalOutput" if name in dbg_names else "Internal"
        return nc.dram_tensor(name, list(shape), dt, kind=kind).ap()

    x_in = din("x", [BPC, SEQ, D])
    ctx_in = din("ctx", [BPC, CTX, D])
    cT_in = din("cT", [128, 16, 3])
    ident_in = din("ident", [128, 128])
    perm_in = din("permrope", [128, 128])
    ropec_in = din("ropecos", [128, SEQ])
    ropes_in = din("ropesin", [128, SEQ])
    masks_in = din("masks", [128, 7, 128])
    ccsc_in = din("ccsc", [128, 2, 512], BF16)
    c256_in = din("c256", [128, 2, 512], BF16)
    dft_in = din("dft", [4, 128, 16, 1024], BF16)
    nabias_in = din("nabias", [L, NHEAD, 128, 5, 576])
    cw_in = din("convw", [L, 128, 24, 5])
    cb_in = din("convb", [L, 128, 24])
    vecs_in = din("vecs", [L, 8, D])
    w_mod = din("w_mod", [L, D, 6 * D])
    w_in = din("w_in", [L, D, PROJ])
    w_br = din("w_br", [L, 4096, D])
    w_out = din("w_out", [L, D, D])
    w_g = din("w_ffn_gate", [L, D, FFN])
    w_u = din("w_ffn_up", [L, D, FFN])
    w_d = din("w_ffn_down", [L, FFN, D])
    out_d = nc.dram_tensor("out", [BPC, SEQ, D], F32, kind="ExternalOutput").ap()

    QT = dscr("QT", [BPC, 1024, TB], BF16)
    KT = dscr("KT", [BPC, 1024, TB], BF16)
    VV = dscr("VV", [BPC, TB, 1024], BF16)
    UT = dscr("UT", [BPC, 1024, TB], BF16)
    XBCT = dscr("XBCT", [BPC, 3072, TB], F32)
    ZZ = dscr("ZZ", [BPC, TB, 2048], F32)
    DTT = dscr("DTT", [BPC, TB, 64], F32)
    GT = dscr("GT", [BPC, 6144, TB], F32)
    MODR = dscr("MODR", [3, 2 * D], F32)
    XS = dscr("XS", [BPC, SEQ, D], F32)
    X1 = dscr("X1", [BPC, SEQ, D], F32)
    CS = dscr("CS", [BPC, CTX, D], F32)
    C1 = dscr("C1", [BPC, CTX, D], F32)
    OT = dscr("OT", [BPC, 4096, TB], BF16)
    XSTM = dscr("XSTM", [BPC, TB, 2048], BF16)
    BTM = dscr("BTM", [BPC, TB, 512], BF16)
    BCT = dscr("BCT", [BPC, 1024, TB], BF16)
    YS = dscr("YS", [BPC, TB, 2048], F32)
    MT = dscr("MT", [BPC, 2048, TB], BF16)
    WGU = dscr("WGU", [2, 22, 128, 16 * 256], BF16)
    WD = dscr("WD", [16, 128, 22 * 256], BF16)

    es0 = ExitStack()

    uniq = [0]

    def sbt(es, name, shape, dt):
        uniq[0] += 1
        return Tl(es.enter_context(nc.sbuf_tensor("s%d_%s" % (uniq[0], name), list(shape), dt)), name)

    def pst(es, name, shape, dt=F32):
        uniq[0] += 1
        return Tl(es.enter_context(nc.psum_tensor("p%d_%s" % (uniq[0], name), list(shape), dt)), name)

    def dma(q, t, out, in_, reads=(), writes=()):
        S.dma(q, t.name, out, in_, reads=[x.r for x in reads], writes=[x.r for x in writes])

    def load(t, out, in_):
        dma("sp", t, out, in_, writes=[t])

    def store(t, out, in_):
        dma("act", t, out, in_, reads=[t])

    def op(e, fn, reads=(), writes=()):
        S.op(e, fn, [x.r for x in reads], [x.r for x in writes])

    def V(fn, reads, writes):
        op("dve", fn, reads, writes)

    def A(fn, reads, writes):
        op("act", fn, reads, writes)

    def P(fn, reads, writes):
        op("pe", fn, reads, writes)

    def G(fn, reads, writes):
        op("pool", fn, reads, writes)

    ev_i = [0]

    def evac(out, in_, reads, writes, scale=None):
        ev_i[0] += 1
        if ev_i[0] % 2:
            if scale is None:
                V(lambda: nc.vector.tensor_copy(out=out, in_=in_), reads, writes)
            else:
                V(lambda: nc.vector.tensor_scalar(out=out, in0=in_, scalar1=float(scale), scalar2=None, op0=ALU.mult), reads, writes)
        else:
            if scale is None:
                A(lambda: nc.scalar.copy(out=out, in_=in_), reads, writes)
            else:
                A(lambda: nc.scalar.activation(out=out, in_=in_, func=AF.Copy, scale=float(scale)), reads, writes)

    def mm(out, lhsT, rhs, start, stop):
        return nc.tensor.matmul(out, lhsT, rhs, start=start, stop=stop)

    eps_t = sbt(es0, "eps_t", [128, 2], F32)
    ident_f = sbt(es0, "ident_f", [128, 128], F32)
    ident_b = sbt(es0, "ident_b", [128, 128], BF16)
    cact = sbt(es0, "cact", [128, 16, 3], F32)
    modT = sbt(es0, "modT", [128, 64, 3], F32)
    stats_r = Rot([sbt(es0, "stats%d" % i, [128, 4, 6], F32) for i in range(4)])
    mv_r = Rot([sbt(es0, "mv%d" % i, [128, 8], F32) for i in range(4)])

    V(lambda: nc.vector.memset(eps_t.a[:, 0:1], 1e-6), [], [eps_t])
    V(lambda: nc.vector.memset(eps_t.a[:, 1:2], 1e-5), [], [eps_t])
    load(ident_f, ident_f.a[:], ident_in)
    V(lambda: nc.vector.tensor_copy(out=ident_b.a[:], in_=ident_f.a[:]), [ident_f], [ident_b])
    load(cact, cact.a[:], cT_in)
    A(lambda: nc.scalar.activation(out=cact.a[:], in_=cact.a[:], func=AF.Silu), [cact], [cact])

    def ln_stats_g(src, srct, eps_col):
        stats = stats_r.next()
        mv = mv_r.next()
        for j in range(4):
            V(lambda j=j: nc.vector.bn_stats(out=stats.a[:, j, :], in_=src[:, j * 512:(j + 1) * 512]), [srct], [stats])
        yield
        V(lambda: nc.vector.bn_aggr(out=mv.a[:, 0:2], in_=stats.a[:].rearrange("p a b -> p (a b)")), [stats], [mv])
        yield
        A(lambda: nc.scalar.activation(out=mv.a[:, 2:3], in_=mv.a[:, 1:2], func=AF.Sqrt, bias=eps_t.a[:, eps_col:eps_col + 1], scale=1.0),
          [mv, eps_t], [mv])
        yield
        V(lambda: nc.vector.reciprocal(out=mv.a[:, 3:4], in_=mv.a[:, 2:3]), [mv], [mv])
        yield
        V(lambda: nc.vector.tensor_scalar(out=mv.a[:, 4:5], in0=mv.a[:, 0:1], scalar1=mv.a[:, 3:4], scalar2=-1.0,
                                          op0=ALU.mult, op1=ALU.mult), [mv], [mv])
        yield
        return mv

    cast_sel = [0]

    class WStream:
        def __init__(self, es, nk, ncols, tag):
            self.nk, self.ncols = nk, ncols
            self.st = Rot([sbt(es, "wst%s%d" % (tag, i), [128, nk, ncols], F32) for i in range(2)])
            self.bf = Rot([sbt(es, "wbf%s%d" % (tag, i), [128, nk, ncols], BF16) for i in range(2)])
            self.q = []

        def issue(self, wap, r0, nk, c0, ncols):
            t = self.st.next()
            src = wap[r0:r0 + nk * 128, c0:c0 + ncols].rearrange("(k p) n -> p k n", p=128)
            load(t, t.a[:, 0:nk, 0:ncols], src)
            self.q.append((t, nk, ncols))

        def get(self, cast=True):
            t, nk, ncols = self.q.pop(0)
            if not cast:
                return t
            b = self.bf.next()
            cast_sel[0] += 1
            if cast_sel[0] % 2:
                A(lambda: nc.scalar.copy(out=b.a[:, 0:nk, 0:ncols], in_=t.a[:, 0:nk, 0:ncols]), [t], [b])
            else:
                V(lambda: nc.vector.tensor_copy(out=b.a[:, 0:nk, 0:ncols], in_=t.a[:, 0:nk, 0:ncols]), [t], [b])
            return b

    def run_groups(ws, groups, body, cast=True):
        if not groups:
            return
        n = len(groups)
        ws.issue(*groups[0][:5])
        if n > 1:
            ws.issue(*groups[1][:5])
        cur = ws.get(cast)
        for i, g in enumerate(groups):
            nxt = None
            if i + 1 < n:
                if cast:
                    nxt = ws.get(cast)
                    if i + 2 < n:
                        ws.issue(*groups[i + 2][:5])
                else:
                    nxt = ws.get(cast)
                    if i + 2 < n:
                        pass
            body(cur, g)
            if not cast and i + 2 < n:
                ws.issue(*groups[i + 2][:5])
            cur = nxt

    for li in range(L):
        l = li
        lg = first_layer_idx + li
        last = (lg == DEPTH - 1)
        xcur = x_in if li == 0 else XS
        ccur = ctx_in if li == 0 else CS
        xout = out_d if li == L - 1 else XS
        with ExitStack() as es:
            ws = WStream(es, 16, 512, "m")
            pb = Rot([pst(es, "pbm%d" % i, [128, 512]) for i in range(4)])
            stg = Rot([sbt(es, "stgm%d" % i, [128, 512], F32) for i in range(2)])
            groups = []
            for which, slot in [(0, 0), (1, 1), (3, 2), (4, 3)]:
                for g in range(4):
                    groups.append((w_mod[l], 0, 16, which * D + g * 512, 512, ("fm", slot * 16 + g * 4)))
            for gi, which in enumerate((2, 5)):
                for g in range(4):
                    groups.append((w_mod[l], 0, 16, which * D + g * 512, 512, ("row", gi * D + g * 512)))

            def body(wt, g):
                kind, dst = g[5]
                p = pb.next()
                if kind == "fm":
                    for m in range(4):
                        for k in range(16):
                            P(lambda m=m, k=k: mm(p.a[:, m * 4:m * 4 + 3], wt.a[:, k, m * 128:(m + 1) * 128], cact.a[:, k, :], k == 0, k == 15),
                              [wt, cact], [p])
                    V(lambda: nc.vector.tensor_copy(out=modT.a[:, dst:dst + 4, :],
                                                    in_=p.a[:, 0:16].rearrange("p (m r) -> p m r", r=4)[:, :, 0:3]), [p], [modT])
                else:
                    for k in range(16):
                        P(lambda k=k: mm(p.a[0:3, :], cact.a[:, k, :], wt.a[:, k, :], k == 0, k == 15), [wt, cact], [p])
                    s_ = stg.next()
                    V(lambda: nc.vector.tensor_copy(out=s_.a[0:3, :], in_=p.a[0:3, :]), [p], [s_])
                    store(s_, MODR[:, dst:dst + 512], s_.a[0:3, :])
            run_groups(ws, groups, body, cast=False)
            for slot in (1, 3):
                V(lambda slot=slot: nc.vector.tensor_scalar(out=modT.a[:, slot * 16:(slot + 1) * 16, :], in0=modT.a[:, slot * 16:(slot + 1) * 16, :],
                                                            scalar1=1.0, scalar2=None, op0=ALU.add), [modT], [modT])
        S.barrier()

        def modulate_g(es_bufs, src, srct, hT, tcol, row, sh_slot, sc_slot):
            xn, ptp = es_bufs
            mv = yield from ln_stats_g(src, srct, 0)
            xnt = xn.next()
            A(lambda: nc.scalar.activation(out=xnt.a[:], in_=src, func=AF.Identity, bias=mv.a[:, 4:5], scale=mv.a[:, 3:4]), [srct, mv], [xnt])
            yield
            for q4 in range(4):
                p = ptp.next()
                for c4 in range(4):
                    c = q4 * 4 + c4
                    P(lambda c=c, c4=c4: nc.tensor.transpose(p.a[:, c4 * 128:(c4 + 1) * 128], xnt.a[:, c * 128:(c + 1) * 128], ident_f.a[:]),
                      [xnt, ident_f], [p])
                for c4 in range(4):
                    c = q4 * 4 + c4
                    V(lambda c=c, c4=c4: nc.vector.tensor_scalar(
                        out=hT.a[:, c, tcol:tcol + 128], in0=p.a[:, c4 * 128:(c4 + 1) * 128],
                        scalar1=modT.a[:, sc_slot * 16 + c, row:row + 1], scalar2=modT.a[:, sh_slot * 16 + c, row:row + 1],
                        op0=ALU.mult, op1=ALU.add), [p, modT], [hT])
                yield

        with ExitStack() as es:
            ws = WStream(es, 16, 512, "a")
            hT = sbt(es, "hT", [128, 16, 1024], BF16)
            xt = Rot([sbt(es, "xt%d" % i, [128, D], F32) for i in range(3)])
            xn = Rot([sbt(es, "xn%d" % i, [128, D], F32) for i in range(3)])
            ptp = Rot([pst(es, "ptp%d" % i, [128, 512]) for i in range(2)])
            pb = Rot([pst(es, "pba%d" % i, [128, 512]) for i in range(6)])
            stg = Rot([sbt(es, "stga%d" % i, [128, 512], F32) for i in range(4)])
            stgb = Rot([sbt(es, "stgab%d" % i, [128, 512], BF16) for i in range(4)])
            blocks = [("ctx", 0, 0)] + [("lat", b, h) for b in range(BPC) for h in range(2)]
            for (bk, bb, bh) in blocks:
                ntile = 4 if bk == "ctx" else 8
                NT = ntile * 128

                def tile_src(t):
                    if bk == "ctx":
                        return ccur[t // 2, (t % 2) * 128:(t % 2 + 1) * 128, :]
                    return xcur[bb, bh * 1024 + t * 128: bh * 1024 + (t + 1) * 128, :]
                row = 2 if bk == "ctx" else bb
                def a1_tile(t):
                    tl = xt.next()
                    load(tl, tl.a[:], tile_src(t))
                    yield
                    yield from modulate_g((xn, ptp), tl.a[:], tl, hT, t * 128, row, 0, 1)
                for t0_ in range(0, ntile, 3):
                    interleave([a1_tile(t) for t in range(t0_, min(t0_ + 3, ntile))])

                def dst_fm(T_, f0):
                    if bk == "ctx":
                        return [(T_[0, f0:f0 + 128, 0:256], 0, 256), (T_[1, f0:f0 + 128, 0:256], 256, 256)]
                    t0 = CTX + bh * 1024
                    return [(T_[bb, f0:f0 + 128, t0:t0 + 1024], 0, 1024)]
                need_all = not (last and bk == "ctx")
                fm_list = []
                if need_all:
                    fm_list += [(QT, C_Q, 1024, BF16)]
                fm_list += [(KT, C_K, 1024, BF16)]
                if need_all:
                    fm_list += [(UT, C_U, 1024, BF16)]
                fm_list += [(XBCT, C_XBC, 3072, F32)]
                if need_all:
                    fm_list += [(GT, C_G, 6144, F32)]
                groups = []
                for (T_, c0, n, dt_) in fm_list:
                    for g in range(n // 512):
                        groups.append((w_in[l], 0, 16, c0 + g * 512, 512, ("fm", T_, g * 512, dt_)))
                tm_list = [(VV, C_V, 1024, BF16)]
                if need_all:
                    tm_list += [(ZZ, C_Z, 2048, F32)]
                for (T_, c0, n, dt_) in tm_list:
                    for g in range(n // 512):
                        groups.append((w_in[l], 0, 16, c0 + g * 512, 512, ("tm", T_, g * 512, dt_)))
                groups.append((w_in[l], 0, 16, C_DT, 64, ("tm", DTT, 0, F32)))

                def body(wt, g):
                    ncols = g[4]
                    kind, T_, d0, dt_ = g[5]
                    if kind == "fm":
                        for m in range(4):
                            dsts = dst_fm(T_, d0 + m * 128)
                            for n in range(NT // 512):
                                p = pb.next()
                                for k in range(16):
                                    P(lambda m=m, k=k, n=n: mm(p.a[:, :], wt.a[:, k, m * 128:(m + 1) * 128], hT.a[:, k, n * 512:(n + 1) * 512], k == 0, k == 15),
                                      [wt, hT], [p])
                                s_ = stgb.next() if dt_ == BF16 else stg.next()
                                evac(s_.a[:, :], p.a[:, :], [p], [s_])
                                for (dap, cc0, cn) in dsts:
                                    lo = max(cc0, n * 512)
                                    hi = min(cc0 + cn, (n + 1) * 512)
                                    if lo < hi:
                                        store(s_, dap[:, lo - cc0:hi - cc0], s_.a[:, lo - n * 512:hi - n * 512])
                    else:
                        for t in range(ntile):
                            p = pb.next()
                            for k in range(16):
                                P(lambda t=t, k=k: mm(p.a[:, 0:ncols], hT.a[:, k, t * 128:(t + 1) * 128], wt.a[:, k, 0:ncols], k == 0, k == 15),
                                  [wt, hT], [p])
                            s_ = stgb.next() if dt_ == BF16 else stg.next()
                            evac(s_.a[:, 0:ncols], p.a[:, 0:ncols], [p], [s_])
                            if bk == "ctx":
                                dap = T_[t // 2, (t % 2) * 128:(t % 2 + 1) * 128, d0:d0 + ncols]
                            else:
                                tk = CTX + bh * 1024 + t * 128
                                dap = T_[bb, tk:tk + 128, d0:d0 + ncols]
                            store(s_, dap, s_.a[:, 0:ncols])
                run_groups(ws, groups, body)
        S.barrier()
        if stop_after == "A":
            break
        with ExitStack() as es:
            permf = sbt(es, "permf", [128, 128], F32)
            permb = sbt(es, "permb", [128, 128], BF16)
            rc = sbt(es, "ropec", [128, SEQ], F32)
            rs = sbt(es, "ropes", [128, SEQ], F32)
            load(permf, permf.a[:], perm_in)
            V(lambda: nc.vector.tensor_copy(out=permb.a[:], in_=permf.a[:]), [permf], [permb])
            load(rc, rc.a[:], ropec_in)
            load(rs, rs.a[:], ropes_in)
            bias = Rot([sbt(es, "nab%d" % i, [128, 5, 576], F32) for i in range(2)])
            qT = Rot([sbt(es, "qT%d" % i, [128, TB], BF16) for i in range(2)])
            kT = Rot([sbt(es, "kT%d" % i, [128, TB], BF16) for i in range(2)])
            vt = Rot([sbt(es, "vt%d" % i, [128, 18, 128], BF16) for i in range(2)])
            qr = sbt(es, "qrot", [128, SEQ], BF16)
            kr = sbt(es, "krot", [128, SEQ], BF16)
            oT = Rot([sbt(es, "oT%d" % i, [128, TB], BF16) for i in range(2)])
            tmp1 = Rot([sbt(es, "rtmp%d" % i, [128, 512], F32) for i in range(2)])
            tmp2 = Rot([sbt(es, "rtmq%d" % i, [128, 512], F32) for i in range(2)])
            sful = Rot([sbt(es, "sful%d" % i, [128, 832], F32) for i in range(2)])
            pexp = Rot([sbt(es, "pexp%d" % i, [128, 832], BF16) for i in range(2)])
            pnrm = Rot([sbt(es, "pnrm%d" % i, [128, 832], BF16) for i in range(2)])
            pTs = Rot([sbt(es, "pTs%d" % i, [128, 896], BF16) for i in range(2)])
            sm = Rot([sbt(es, "sm%d" % i, [128, 4], F32) for i in range(4)])
            psA = Rot([pst(es, "psA%d" % i, [128, 512]) for i in range(2)])
            psB = Rot([pst(es, "psB%d" % i, [128, 512]) for i in range(2)])
            psT = Rot([pst(es, "psT%d" % i, [128, 1024], BF16) for i in range(2)])
            psO = Rot([pst(es, "psO%d" % i, [128, 512]) for i in range(2)])
            SCALE = float(HD) ** -0.5

            def softmax_pv(sf, W, blocks, vtl, o_t, ocol):
                st = sm.next()
                V(lambda: nc.vector.tensor_reduce(out=st.a[:, 0:1], in_=sf.a[:, 0:W], axis=AX.X, op=ALU.max), [sf], [st])
                V(lambda: nc.vector.tensor_scalar(out=st.a[:, 1:2], in0=st.a[:, 0:1], scalar1=-1.0, scalar2=None, op0=ALU.mult), [st], [st])
                yield
                pe_ = pexp.next()
                A(lambda: nc.scalar.activation(out=pe_.a[:, 0:W], in_=sf.a[:, 0:W], func=AF.Exp, bias=st.a[:, 1:2], scale=1.0, accum_out=st.a[:, 2:3]),
                  [sf, st], [pe_, st])
                yield
                V(lambda: nc.vector.reciprocal(out=st.a[:, 3:4], in_=st.a[:, 2:3]), [st], [st])
                pn = pnrm.next()
                V(lambda: nc.vector.tensor_scalar(out=pn.a[:, 0:W], in0=pe_.a[:, 0:W], scalar1=st.a[:, 3:4], scalar2=None, op0=ALU.mult), [pe_, st], [pn])
                yield
                pt = psT.next()
                for bi, (c0, nk, vti) in enumerate(blocks):
                    P(lambda bi=bi, c0=c0, nk=nk: nc.tensor.transpose(pt.a[0:nk, bi * 128:(bi + 1) * 128], pn.a[:, c0:c0 + nk], ident_b.a[:]),
                      [pn, ident_b], [pt])
                yield
                ps_ = pTs.next()
                nfull = sum(1 for b_ in blocks if b_[1] == 128)
                evac(ps_.a[:, 0:nfull * 128], pt.a[:, 0:nfull * 128], [pt], [ps_])
                if nfull < len(blocks):
                    evac(ps_.a[0:64, nfull * 128:(nfull + 1) * 128], pt.a[0:64, nfull * 128:(nfull + 1) * 128], [pt], [ps_])
                yield
                po = psO.next()
                for bi, (c0, nk, vti) in enumerate(blocks):
                    P(lambda bi=bi, nk=nk, vti=vti: mm(po.a[:, 0:128], vtl.a[0:nk, vti, :], ps_.a[0:nk, bi * 128:(bi + 1) * 128], bi == 0, bi == len(blocks) - 1),
                      [vtl, ps_], [po])
                yield
                evac(o_t.a[:, ocol:ocol + 128], po.a[:, 0:128], [po], [o_t])

            for h in range(NHEAD):
                bt = bias.next()
                load(bt, bt.a[:], nabias_in[l, h])
                for b in range(BPC):
                    q_ = qT.next(); k_ = kT.next(); v_ = vt.next(); o_ = oT.next()
                    if not last:
                        load(q_, q_.a[:], QT[b, h * 128:(h + 1) * 128, :])
                    else:
                        load(q_, q_.a[:, CTX:], QT[b, h * 128:(h + 1) * 128, CTX:])
                    load(k_, k_.a[:], KT[b, h * 128:(h + 1) * 128, :])
                    load(v_, v_.a[:], VV[b, :, h * 128:(h + 1) * 128].rearrange("(t p) d -> p t d", p=128))
                    for (src, dst) in ((q_, qr), (k_, kr)):
                        for n in range(4):
                            pp = psA.next()
                            P(lambda n=n: mm(pp.a[:, :], permb.a[:], src.a[:, CTX + n * 512:CTX + (n + 1) * 512], True, True), [permb, src], [pp])
                            t1 = tmp1.next(); t2 = tmp2.next()
                            V(lambda n=n: nc.vector.tensor_tensor(out=t1.a[:], in0=src.a[:, CTX + n * 512:CTX + (n + 1) * 512], in1=rc.a[:, n * 512:(n + 1) * 512], op=ALU.mult),
                              [src, rc], [t1])
                            V(lambda n=n: nc.vector.tensor_tensor(out=t2.a[:], in0=pp.a[:, :], in1=rs.a[:, n * 512:(n + 1) * 512], op=ALU.mult), [pp, rs], [t2])
                            V(lambda n=n: nc.vector.tensor_tensor(out=dst.a[:, n * 512:(n + 1) * 512], in0=t1.a[:], in1=t2.a[:], op=ALU.add), [t1, t2], [dst])
                    def lat_tile(j, q_=q_, k_=k_, v_=v_, o_=o_, bt=bt):
                        if j < 2:
                            start, nrows, typ = 0, 9, j
                        elif j < 14:
                            start, nrows, typ = 2 * j - 4, 9, 2
                        else:
                            start, nrows, typ = 24, 8, j - 11
                        W = 768 + (64 if nrows == 9 else 0)
                        pa = psA.next(); pb_ = psB.next()
                        P(lambda: mm(pa.a[:, :], qr.a[:, j * 128:(j + 1) * 128], kr.a[:, start * 64:start * 64 + 512], True, True), [qr, kr], [pa])
                        P(lambda: mm(pb_.a[:, 0:256], q_.a[:, CTX + j * 128:CTX + (j + 1) * 128], k_.a[:, 0:CTX], True, True), [q_, k_], [pb_])
                        if nrows == 9:
                            P(lambda: mm(pb_.a[:, 256:320], qr.a[:, j * 128:(j + 1) * 128], kr.a[:, start * 64 + 512:start * 64 + 576], True, True),
                              [qr, kr], [pb_])
                        yield
                        sf = sful.next()
                        V(lambda: nc.vector.scalar_tensor_tensor(out=sf.a[:, 0:512], in0=pa.a[:, :], scalar=SCALE, in1=bt.a[:, typ, 0:512], op0=ALU.mult, op1=ALU.add),
                          [pa, bt], [sf])
                        A(lambda: nc.scalar.activation(out=sf.a[:, 512:768], in_=pb_.a[:, 0:256], func=AF.Copy, scale=SCALE), [pb_], [sf])
                        if nrows == 9:
                            V(lambda: nc.vector.scalar_tensor_tensor(out=sf.a[:, 768:832], in0=pb_.a[:, 256:320], scalar=SCALE, in1=bt.a[:, typ, 512:576], op0=ALU.mult, op1=ALU.add),
                              [pb_, bt], [sf])
                        yield
                        blocks = [(i * 128, 128, 2 + start // 2 + i) for i in range(4)] + [(512, 128, 0), (640, 128, 1)]
                        if nrows == 9:
                            blocks.append((768, 64, 2 + start // 2 + 4))
                        yield from softmax_pv(sf, W, blocks, v_, o_, CTX + j * 128)
                    for j in range(0, 16, 2):
                        interleave([lat_tile(j), lat_tile(j + 1)])
                    if not last:
                        def ctx_tile(j, q_=q_, k_=k_, v_=v_, o_=o_):
                            pb_ = psB.next()
                            P(lambda: mm(pb_.a[:, 0:256], q_.a[:, j * 128:(j + 1) * 128], k_.a[:, 0:CTX], True, True), [q_, k_], [pb_])
                            yield
                            sf = sful.next()
                            A(lambda: nc.scalar.activation(out=sf.a[:, 0:256], in_=pb_.a[:, 0:256], func=AF.Copy, scale=SCALE), [pb_], [sf])
                            yield
                            yield from softmax_pv(sf, 256, [(0, 128, 0), (128, 128, 1)], v_, o_, j * 128)
                        interleave([ctx_tile(0), ctx_tile(1)])
                        store(o_, OT[b, h * 128:(h + 1) * 128, :], o_.a[:])
                    else:
                        store(o_, OT[b, h * 128:(h + 1) * 128, CTX:], o_.a[:, CTX:])
        S.barrier()
        if stop_after == "B":
            break
        with ExitStack() as es:
            ccsc = sbt(es, "ccsc", [128, 2, 512], BF16)
            c256 = sbt(es, "c256", [128, 2, 512], BF16)
            load(ccsc, ccsc.a[:], ccsc_in)
            load(c256, c256.a[:], c256_in)
            AB = sbt(es, "AB", [128, 18, 4, 512], BF16)
            pb = Rot([pst(es, "pbc%d" % i, [128, 512]) for i in range(6)])
            stgb = Rot([sbt(es, "stgcb%d" % i, [128, 512], BF16) for i in range(4)])
            SC_LAT = float(SEQ * 256) ** -0.5
            SC_CTX = float(CTX * 256) ** -0.5
            for b in range(BPC):
                t0 = 0 if not last else 2
                with ExitStack() as es2:
                    uT = sbt(es2, "uT", [128, 8, TB], BF16)
                    lo = 0 if not last else CTX
                    load(uT, uT.a[:, :, lo:], UT[b, :, lo:].rearrange("(c p) t -> p c t", p=128))
                    for t in range(t0, 18):
                        for g in range(4):
                            p = pb.next()
                            for kk in range(2):
                                P(lambda t=t, g=g, kk=kk: mm(p.a[:, :], uT.a[:, g * 2 + kk, t * 128:(t + 1) * 128], ccsc.a[:, kk, :], kk == 0, kk == 1),
                                  [uT, ccsc], [p])
                            evac(AB.a[:, t, g, :], p.a[:, :], [p], [AB])
                S.barrier()
                with ExitStack() as es2:
                    dft = Rot([sbt(es2, "dftb%d" % i, [128, 16, 1024], BF16) for i in range(2)])
                    nxt = dft.next()
                    load(nxt, nxt.a[:], dft_in[0])
                    for sb_ in range(4):
                        cur = nxt
                        if sb_ + 1 < 4:
                            nxt = dft.next()
                            load(nxt, nxt.a[:], dft_in[sb_ + 1])
                        for g in range(4):
                            for a in range(2):
                                p = pb.next()
                                for st_ in range(16):
                                    P(lambda g=g, a=a, st_=st_: mm(p.a[:, :], AB.a[:, 2 + st_, g, a * 128:(a + 1) * 128], cur.a[:, st_, 0:512], st_ == 0, False),
                                      [AB, cur], [p])
                                    P(lambda g=g, a=a, st_=st_: mm(p.a[:, :], AB.a[:, 2 + st_, g, 256 + a * 128:256 + (a + 1) * 128], cur.a[:, st_, 512:1024], False, st_ == 15),
                                      [AB, cur], [p])
                                s_ = stgb.next()
                                evac(s_.a[:, :], p.a[:, :], [p], [s_], scale=SC_LAT)
                                f0 = 1024 + g * 256 + a * 128
                                store(s_, OT[b, f0:f0 + 128, CTX + sb_ * 512:CTX + (sb_ + 1) * 512], s_.a[:, :])
                    if not last:
                        for g in range(4):
                            for a in range(2):
                                p = pb.next()
                                for st_ in range(2):
                                    P(lambda g=g, a=a, st_=st_: mm(p.a[:, 0:256], AB.a[:, st_, g, a * 128:(a + 1) * 128], c256.a[:, st_, 0:256], st_ == 0, False),
                                      [AB, c256], [p])
                                    P(lambda g=g, a=a, st_=st_: mm(p.a[:, 0:256], AB.a[:, st_, g, 256 + a * 128:256 + (a + 1) * 128], c256.a[:, st_, 256:512], False, st_ == 1),
                                      [AB, c256], [p])
                                s_ = stgb.next()
                                evac(s_.a[:, 0:256], p.a[:, 0:256], [p], [s_], scale=SC_CTX)
                                f0 = 1024 + g * 256 + a * 128
                                store(s_, OT[b, f0:f0 + 128, 0:CTX], s_.a[:, 0:256])
                S.barrier()
        S.barrier()
        if stop_after == "C":
            break
        with ExitStack() as es:
            cw = sbt(es, "cw", [128, 24, 5], F32)
            cb = sbt(es, "cb", [128, 24], F32)
            load(cw, cw.a[:], cw_in[l])
            load(cb, cb.a[:], cb_in[l])
            xin = Rot([sbt(es, "cxin%d" % i, [128, TB], F32) for i in range(2)])
            acc = Rot([sbt(es, "cacc%d" % i, [128, TB], F32) for i in range(2)])
            xo = Rot([sbt(es, "cxo%d" % i, [128, TB], BF16) for i in range(2)])
            tmo = Rot([sbt(es, "ctmo%d" % i, [128, 18, 128], BF16) for i in range(2)])
            ptc = Rot([pst(es, "ptc%d" % i, [128, 1024], BF16) for i in range(4)])
            for b in range(BPC):
                for cc in range(24):
                    xi = xin.next()
                    load(xi, xi.a[:], XBCT[b, cc * 128:(cc + 1) * 128, :])
                    ac = acc.next()
                    eng = V
                    ne = nc.vector
                    eng(lambda: ne.tensor_scalar(out=ac.a[:], in0=xi.a[:], scalar1=cw.a[:, cc, 2:3], scalar2=None, op0=ALU.mult), [xi, cw], [ac])
                    for (s0, s1) in ((0, CTX), (CTX, TB)):
                        for j in (0, 1, 3, 4):
                            sh = j - 2
                            if sh < 0:
                                o_sl = (s0 - sh, s1); i_sl = (s0, s1 + sh)
                            else:
                                o_sl = (s0, s1 - sh); i_sl = (s0 + sh, s1)
                            eng(lambda j=j, o_sl=o_sl, i_sl=i_sl: ne.scalar_tensor_tensor(
                                out=ac.a[:, o_sl[0]:o_sl[1]], in0=xi.a[:, i_sl[0]:i_sl[1]], scalar=cw.a[:, cc, j:j + 1],
                                in1=ac.a[:, o_sl[0]:o_sl[1]], op0=ALU.mult, op1=ALU.add), [xi, cw, ac], [ac])
                    xo_ = xo.next()
                    A(lambda: nc.scalar.activation(out=xo_.a[:], in_=ac.a[:], func=AF.Silu, bias=cb.a[:, cc:cc + 1], scale=1.0), [ac, cb], [xo_])
                    if cc >= 16:
                        store(xo_, BCT[b, (cc - 16) * 128:(cc - 15) * 128, :], xo_.a[:])
                    if cc < 20:
                        tm = tmo.next()
                        for t8 in range(0, 18, 8):
                            nt = min(8, 18 - t8)
                            p = ptc.next()
                            for t in range(nt):
                                P(lambda t=t, t8=t8: nc.tensor.transpose(p.a[:, t * 128:(t + 1) * 128], xo_.a[:, (t8 + t) * 128:(t8 + t + 1) * 128], ident_b.a[:]),
                                  [xo_, ident_b], [p])
                            evac(tm.a[:, t8:t8 + nt, :], p.a[:, 0:nt * 128].rearrange("p (t c) -> p t c", c=128), [p], [tm])
                        if cc < 16:
                            dst = XSTM[b, :, cc * 128:(cc + 1) * 128]
                        else:
                            dst = BTM[b, :, (cc - 16) * 128:(cc - 15) * 128]
                        store(tm, dst.rearrange("(t p) c -> p t c", p=128), tm.a[:])
        S.barrier()
        if stop_after == "D1":
            break
        with ExitStack() as es:
            masks = sbt(es, "masks", [128, 7, 128], F32)
            load(masks, masks.a[:], masks_in)
            vb = sbt(es, "vb", [128, 160], F32)
            load(vb, vb.a[:], vecs_in[l, 5:6, 0:160].partition_broadcast(128))
            nwb = sbt(es, "nwb", [128, D], F32)
            load(nwb, nwb.a[:], vecs_in[l, 4:5, :].partition_broadcast(128))
            A(lambda: nc.scalar.activation(out=vb.a[:, 64:128], in_=vb.a[:, 64:128], func=AF.Exp), [vb], [vb])
            V(lambda: nc.vector.tensor_scalar(out=vb.a[:, 64:128], in0=vb.a[:, 64:128], scalar1=-1.0, scalar2=None, op0=ALU.mult), [vb], [vb])
            one_t = sbt(es, "one_t", [128, 1], F32)
            V(lambda: nc.vector.memset(one_t.a[:], 1.0), [], [one_t])
            def mkbufs(ci):
                B_ = {}
                sfx = "_%d" % ci
                B_["xs_r"] = Rot([sbt(es, "sxs%d%s" % (i, sfx), [128, D], BF16) for i in range(2)])
                B_["btm_r"] = Rot([sbt(es, "sbtm%d%s" % (i, sfx), [128, 512], BF16) for i in range(2)])
                B_["bct_r"] = Rot([sbt(es, "sbct%d%s" % (i, sfx), [128, 8, 128], BF16) for i in range(2)])
                B_["dtr_r"] = Rot([sbt(es, "sdtr%d%s" % (i, sfx), [128, 32], F32) for i in range(2)])
                B_["dts"] = sbt(es, "dts" + sfx, [128, 2, 32], F32)
                B_["e3"] = sbt(es, "e3" + sfx, [128, 96], F32)
                B_["xdt"] = sbt(es, "xdt" + sfx, [128, D], BF16)
                B_["xdd"] = sbt(es, "xdd" + sfx, [128, D], BF16)
                B_["hst"] = sbt(es, "hst" + sfx, [128, D], F32)
                B_["hb"] = sbt(es, "hb" + sfx, [128, D], BF16)
                B_["GM"] = sbt(es, "GM" + sfx, [128, 4, 128], F32)
                B_["lt8"] = Rot([sbt(es, "lt8%d%s" % (i, sfx), [128, 8, 128], F32) for i in range(1)])
                B_["ltmp"] = Rot([sbt(es, "ltmp%d%s" % (i, sfx), [128, 512], F32) for i in range(2)])
                B_["Wt"] = Rot([sbt(es, "Wt%d%s" % (i, sfx), [128, 4, 128], BF16) for i in range(2)])
                B_["ycur_r"] = Rot([sbt(es, "ycur%d%s" % (i, sfx), [128, D], F32) for i in range(1)])
                B_["yprev_r"] = Rot([sbt(es, "yprev%d%s" % (i, sfx), [128, D], F32) for i in range(1)])
                B_["z_r"] = Rot([sbt(es, "zt%d%s" % (i, sfx), [128, D], F32) for i in range(1)])
                B_["obf"] = sbt(es, "obf" + sfx, [128, D], BF16)
                B_["oTt"] = Rot([sbt(es, "oTt%d%s" % (i, sfx), [128, 16, 128], BF16) for i in range(1)])
                B_["ssq"] = sbt(es, "ssq" + sfx, [128, 8], F32)
                return B_
            pcs = pst(es, "pcs", [128, 512])
            pg = pst(es, "pg", [128, 512])
            pd = Rot([pst(es, "pd%d" % i, [128, 512]) for i in range(2)])
            pyd = pst(es, "pyd", [128, 512])
            pyo = pst(es, "pyo", [128, 512])
            pss = pst(es, "pss", [128, 512])
            pto = pst(es, "pto", [128, 1024], BF16)

            def bc_last(ap, n, m):
                return ap.unsqueeze(2).to_broadcast([128, n, m])

            def bc_mid(ap, m, n):
                return ap.unsqueeze(1).to_broadcast([128, m, n])

            def chain(b, B_):
                xs_r, btm_r, bct_r, dtr_r = B_["xs_r"], B_["btm_r"], B_["bct_r"], B_["dtr_r"]
                dts, e3, xdt, xdd, hst, hb, GM = B_["dts"], B_["e3"], B_["xdt"], B_["xdd"], B_["hst"], B_["hb"], B_["GM"]
                lt8, ltmp, Wt, ycur_r, yprev_r, z_r = B_["lt8"], B_["ltmp"], B_["Wt"], B_["ycur_r"], B_["yprev_r"], B_["z_r"]
                obf, oTt, ssq = B_["obf"], B_["oTt"], B_["ssq"]
                for dr in range(2):
                    V(lambda: nc.vector.memset(hst.a[:], 0.0), [], [hst])
                    V(lambda: nc.vector.memset(hb.a[:], 0.0), [], [hb])
                    order = list(range(18)) if dr == 0 else [1, 0] + list(range(17, 1, -1))
                    m_acc, m_dec, m_01 = (0, 1, 5) if dr == 0 else (2, 3, 6)
                    for c in order:
                        need_y = (c >= 2) or (not last)
                        tk = c * 128
                        xs_ = xs_r.next(); btm = btm_r.next(); bct = bct_r.next(); dtr = dtr_r.next()
                        load(xs_, xs_.a[:], XSTM[b, tk:tk + 128, :])
                        load(btm, btm.a[:], BTM[b, tk:tk + 128, :])
                        load(bct, bct.a[:], BCT[b, :, tk:tk + 128].rearrange("(g p) t -> p g t", p=128))
                        load(dtr, dtr.a[:], DTT[b, tk:tk + 128, dr * 32:(dr + 1) * 32])
                        yield
                        V(lambda: nc.vector.tensor_tensor(out=dts.a[:, 0, :], in0=dtr.a[:], in1=vb.a[:, dr * 32:(dr + 1) * 32], op=ALU.add), [dtr, vb], [dts])
                        A(lambda: nc.scalar.activation(out=dts.a[:, 0, :], in_=dts.a[:, 0, :], func=AF.Exp), [dts], [dts])
                        A(lambda: nc.scalar.activation(out=dts.a[:, 0, :], in_=dts.a[:, 0, :], func=AF.Ln, bias=one_t.a[:, 0:1], scale=1.0), [dts, one_t], [dts])
                        V(lambda: nc.vector.tensor_tensor(out=dts.a[:, 1, :], in0=dts.a[:, 0, :], in1=vb.a[:, 64 + dr * 32:64 + (dr + 1) * 32], op=ALU.mult), [dts, vb], [dts])
                        yield
                        for i_, mi in enumerate((m_acc, m_dec, 4)):
                            P(lambda i_=i_, mi=mi: mm(pcs.a[:, i_ * 32:(i_ + 1) * 32], masks.a[:, mi, :], dts.a[:, 1, :], True, True), [masks, dts], [pcs])
                        A(lambda: nc.scalar.activation(out=e3.a[:], in_=pcs.a[:, 0:96], func=AF.Exp), [pcs], [e3])
                        yield
                        V(lambda: nc.vector.tensor_tensor(out=xdt.a[:].rearrange("p (h d) -> p h d", d=64), in0=xs_.a[:].rearrange("p (h d) -> p h d", d=64),
                                                          in1=bc_last(dts.a[:, 0, :], 32, 64), op=ALU.mult), [xs_, dts], [xdt])
                        V(lambda: nc.vector.tensor_tensor(out=xdd.a[:].rearrange("p (h d) -> p h d", d=64), in0=xdt.a[:].rearrange("p (h d) -> p h d", d=64),
                                                          in1=bc_last(e3.a[:, 32:64], 32, 64), op=ALU.mult), [xdt, e3], [xdd])
                        yield
                        if need_y:
                            for g in range(4):
                                P(lambda g=g: mm(pg.a[:, g * 128:(g + 1) * 128], bct.a[:, g, :], bct.a[:, 4 + g, :], True, True), [bct], [pg])
                            V(lambda: nc.vector.tensor_tensor(out=GM.a[:], in0=pg.a[:, :].rearrange("p (g l) -> p g l", l=128),
                                                              in1=bc_mid(masks.a[:, m_01, :], 4, 128), op=ALU.mult), [pg, masks], [GM])
                            yield
                            yc = ycur_r.next()
                            for g in range(4):
                                l8 = lt8.next()
                                V(lambda g=g: nc.vector.tensor_tensor(out=l8.a[:], in0=bc_mid(masks.a[:, m_dec, :], 8, 128),
                                                                      in1=bc_last(dts.a[:, 1, g * 8:(g + 1) * 8], 8, 128), op=ALU.mult), [masks, dts], [l8])
                                yield
                                for hf in range(2):
                                    p_ = pd.next()
                                    for r in range(4):
                                        P(lambda r=r, hf=hf: mm(p_.a[:, r * 128:(r + 1) * 128], l8.a[:, hf * 4 + r, :], masks.a[:, m_acc, :], True, True), [l8, masks], [p_])
                                    lt_ = ltmp.next()
                                    A(lambda: nc.scalar.activation(out=lt_.a[:], in_=p_.a[:, :], func=AF.Exp), [p_], [lt_])
                                    w_ = Wt.next()
                                    V(lambda g=g: nc.vector.tensor_tensor(out=w_.a[:], in0=lt_.a[:].rearrange("p (r l) -> p r l", l=128),
                                                                          in1=bc_mid(GM.a[:, g, :], 4, 128), op=ALU.mult), [lt_, GM], [w_])
                                    for r in range(4):
                                        hd_ = g * 8 + hf * 4 + r
                                        P(lambda r=r, hd_=hd_, hf=hf: mm(pyd.a[:, (hf * 4 + r) * 64:(hf * 4 + r + 1) * 64], w_.a[:, r, :], xdt.a[:, hd_ * 64:(hd_ + 1) * 64], True, True),
                                          [w_, xdt], [pyd])
                                P(lambda g=g: mm(pyo.a[:, :], bct.a[:, 4 + g, :], hb.a[:, g * 512:(g + 1) * 512], True, True), [bct, hb], [pyo])
                                V(lambda g=g: nc.vector.tensor_tensor(out=yc.a[:, g * 512:(g + 1) * 512].rearrange("p (h d) -> p h d", d=64),
                                                                      in0=pyo.a[:, :].rearrange("p (h d) -> p h d", d=64),
                                                                      in1=bc_last(e3.a[:, g * 8:(g + 1) * 8], 8, 64), op=ALU.mult), [pyo, e3], [yc])
                                V(lambda g=g: nc.vector.tensor_tensor(out=yc.a[:, g * 512:(g + 1) * 512], in0=pyd.a[:, :], in1=yc.a[:, g * 512:(g + 1) * 512], op=ALU.add),
                                  [pyd, yc], [yc])
                                yield
                        for g in range(4):
                            P(lambda g=g: mm(pss.a[:, :], btm.a[:, g * 128:(g + 1) * 128], xdd.a[:, g * 512:(g + 1) * 512], True, True), [btm, xdd], [pss])
                            V(lambda g=g: nc.vector.tensor_tensor(out=hst.a[:, g * 512:(g + 1) * 512].rearrange("p (h d) -> p h d", d=64),
                                                                  in0=hst.a[:, g * 512:(g + 1) * 512].rearrange("p (h d) -> p h d", d=64),
                                                                  in1=bc_last(e3.a[:, 64 + g * 8:64 + (g + 1) * 8], 8, 64), op=ALU.mult), [hst, e3], [hst])
                            V(lambda g=g: nc.vector.tensor_tensor(out=hst.a[:, g * 512:(g + 1) * 512], in0=pss.a[:, :], in1=hst.a[:, g * 512:(g + 1) * 512], op=ALU.add),
                              [pss, hst], [hst])
                            yield
                        A(lambda: nc.scalar.copy(out=hb.a[:], in_=hst.a[:]), [hst], [hb])
                        yield
                        if not need_y:
                            continue
                        if dr == 0:
                            yp = yprev_r.next()
                            V(lambda: nc.vector.tensor_tensor(out=yp.a[:].rearrange("p (h d) -> p h d", d=64), in0=xs_.a[:].rearrange("p (h d) -> p h d", d=64),
                                                              in1=bc_last(vb.a[:, 128:160], 32, 64), op=ALU.mult), [xs_, vb], [yp])
                            V(lambda: nc.vector.tensor_tensor(out=yc.a[:], in0=yc.a[:], in1=yp.a[:], op=ALU.add), [yc, yp], [yc])
                            store(yc, YS[b, tk:tk + 128, :], yc.a[:])
                        else:
                            yp = yprev_r.next(); zt = z_r.next()
                            load(yp, yp.a[:], YS[b, tk:tk + 128, :])
                            load(zt, zt.a[:], ZZ[b, tk:tk + 128, :])
                            yield
                            V(lambda: nc.vector.tensor_tensor(out=yc.a[:], in0=yc.a[:], in1=yp.a[:], op=ALU.add), [yc, yp], [yc])
                            A(lambda: nc.scalar.activation(out=zt.a[:], in_=zt.a[:], func=AF.Silu), [zt], [zt])
                            V(lambda: nc.vector.tensor_tensor(out=yc.a[:], in0=yc.a[:], in1=zt.a[:], op=ALU.mult), [yc, zt], [yc])
                            yield
                            for g in range(4):
                                A(lambda g=g: nc.scalar.activation(out=yp.a[:, g * 512:(g + 1) * 512], in_=yc.a[:, g * 512:(g + 1) * 512], func=AF.Square,
                                                                   accum_out=ssq.a[:, g:g + 1]), [yc], [yp, ssq])
                            A(lambda: nc.scalar.activation(out=ssq.a[:, 4:8], in_=ssq.a[:, 0:4], func=AF.Sqrt, bias=eps_t.a[:, 1:2], scale=1.0 / 512.0), [ssq, eps_t], [ssq])
                            V(lambda: nc.vector.reciprocal(out=ssq.a[:, 4:8], in_=ssq.a[:, 4:8]), [ssq], [ssq])
                            yield
                            for g in range(4):
                                V(lambda g=g: nc.vector.scalar_tensor_tensor(out=obf.a[:, g * 512:(g + 1) * 512], in0=yc.a[:, g * 512:(g + 1) * 512], scalar=ssq.a[:, 4 + g:5 + g],
                                                                             in1=nwb.a[:, g * 512:(g + 1) * 512], op0=ALU.mult, op1=ALU.mult), [yc, ssq, nwb], [obf])
                            ot = oTt.next()
                            for h8 in range(2):
                                for c8 in range(8):
                                    cc = h8 * 8 + c8
                                    P(lambda cc=cc, c8=c8: nc.tensor.transpose(pto.a[:, c8 * 128:(c8 + 1) * 128], obf.a[:, cc * 128:(cc + 1) * 128], ident_b.a[:]),
                                      [obf, ident_b], [pto])
                                evac(ot.a[:, h8 * 8:(h8 + 1) * 8, :], pto.a[:, :].rearrange("p (c t) -> p c t", t=128), [pto], [ot])
                            store(ot, OT[b, 2048:4096, tk:tk + 128].rearrange("(c p) t -> p c t", p=128), ot.a[:])
            interleave([chain(b, mkbufs(b)) for b in range(BPC)])
        S.barrier()
        if stop_after == "D2":
            break
        with ExitStack() as es:
            ws = WStream(es, 32, 128, "e")
            oTp = sbt(es, "oTp", [128, 32, 1024], BF16)
            gt_r = Rot([sbt(es, "gte%d" % i, [128, 1024], F32) for i in range(6)])
            acc_r = Rot([sbt(es, "acce%d" % i, [128, 1024], F32) for i in range(3)])
            tmp_r = Rot([sbt(es, "tmpe%d" % i, [128, 512], F32) for i in range(2)])
            mo_r = Rot([sbt(es, "moe%d" % i, [128, 1024], BF16) for i in range(2)])
            pb = Rot([pst(es, "pbe%d" % i, [128, 512]) for i in range(6)])
            blocks = [("lat", b, h) for b in range(BPC) for h in range(2)]
            if not last:
                blocks = [("ctx", 0, 0)] + blocks
            KR = [(0, 8), (8, 8), (16, 16)]
            for (bk, bb, bh) in blocks:
                NT = 512 if bk == "ctx" else 1024

                def segs():
                    if bk == "ctx":
                        return [(0, 0, 0, 256), (1, 0, 256, 256)]
                    return [(bb, CTX + bh * 1024, 0, 1024)]
                for (sb_, t0, c0, n) in segs():
                    load(oTp, oTp.a[:, :, c0:c0 + n], OT[sb_, :, t0:t0 + n].rearrange("(c p) t -> p c t", p=128))
                groups = [(w_br[l], 0, 32, fo * 128, 128, fo) for fo in range(16)]

                gate_q = {}

                def prep_gates(fo):
                    lst = []
                    for i in range(3):
                        gt = gt_r.next()
                        for (sb_, t0, c0, n) in segs():
                            load(gt, gt.a[:, c0:c0 + n], GT[sb_, i * 2048 + fo * 128:i * 2048 + (fo + 1) * 128, t0:t0 + n])
                        A(lambda gt=gt: nc.scalar.activation(out=gt.a[:, 0:NT], in_=gt.a[:, 0:NT], func=AF.Sigmoid), [gt], [gt])
                        lst.append(gt)
                    gate_q[fo] = lst
                prep_gates(0)

                def body(wt, g):
                    fo = g[5]
                    if fo + 1 < 16:
                        prep_gates(fo + 1)
                    gts = gate_q.pop(fo)
                    ac = acc_r.next()
                    for i in range(3):
                        gt = gts[i]
                        k0, nk = KR[i]
                        for n_ in range(NT // 512):
                            p = pb.next()
                            for k in range(nk):
                                P(lambda k=k, n_=n_: mm(p.a[:, :], wt.a[:, k0 + k, :], oTp.a[:, k0 + k, n_ * 512:(n_ + 1) * 512], k == 0, k == nk - 1), [wt, oTp], [p])
                            if i == 0:
                                V(lambda n_=n_: nc.vector.tensor_tensor(out=ac.a[:, n_ * 512:(n_ + 1) * 512], in0=p.a[:, :], in1=gt.a[:, n_ * 512:(n_ + 1) * 512], op=ALU.mult), [p, gt], [ac])
                            else:
                                tm = tmp_r.next()
                                V(lambda n_=n_: nc.vector.tensor_tensor(out=tm.a[:], in0=p.a[:, :], in1=gt.a[:, n_ * 512:(n_ + 1) * 512], op=ALU.mult), [p, gt], [tm])
                                V(lambda n_=n_: nc.vector.tensor_tensor(out=ac.a[:, n_ * 512:(n_ + 1) * 512], in0=ac.a[:, n_ * 512:(n_ + 1) * 512], in1=tm.a[:], op=ALU.add), [ac, tm], [ac])
                    mo = mo_r.next()
                    A(lambda: nc.scalar.copy(out=mo.a[:, 0:NT], in_=ac.a[:, 0:NT]), [ac], [mo])
                    for (sb_, t0, c0, n) in segs():
                        store(mo, MT[sb_, fo * 128:(fo + 1) * 128, t0:t0 + n], mo.a[:, c0:c0 + n])
                run_groups(ws, groups, body)
        S.barrier()
        if stop_after == "E":
            break

        def ln_affine_g(tl, gb, bb_, dst):
            mv = yield from ln_stats_g(tl.a[:], tl, 0)
            A(lambda: nc.scalar.activation(out=tl.a[:], in_=tl.a[:], func=AF.Identity, bias=mv.a[:, 4:5], scale=mv.a[:, 3:4]), [tl, mv], [tl])
            yield
            V(lambda: nc.vector.tensor_tensor(out=tl.a[:], in0=tl.a[:], in1=gb.a[:], op=ALU.mult), [tl, gb], [tl])
            yield
            V(lambda: nc.vector.tensor_tensor(out=tl.a[:], in0=tl.a[:], in1=bb_.a[:], op=ALU.add), [tl, bb_], [tl])
            yield
            store(tl, dst, tl.a[:])

        with ExitStack() as es:
            ws = WStream(es, 16, 256, "f")
            mTp = sbt(es, "mTp", [128, 16, 1024], BF16)
            xb = [sbt(es, "xb%d" % i, [128, D], F32) for i in range(8)]
            g1b = sbt(es, "g1b", [128, D], F32)
            lng = sbt(es, "lng", [128, D], F32)
            lnb = sbt(es, "lnb", [128, D], F32)
            load(lng, lng.a[:], vecs_in[l, 0:1, :].partition_broadcast(128))
            load(lnb, lnb.a[:], vecs_in[l, 1:2, :].partition_broadcast(128))
            tmp_r = Rot([sbt(es, "tmpf%d" % i, [128, 256], F32) for i in range(2)])
            pb = Rot([pst(es, "pbf%d" % i, [128, 512]) for i in range(6)])
            blocks = [("lat", b, h) for b in range(BPC) for h in range(2)]
            if not last:
                blocks = [("ctx", 0, 0)] + blocks
            for (bk, bb, bh) in blocks:
                ntile = 4 if bk == "ctx" else 8
                row = 2 if bk == "ctx" else bb
                load(g1b, g1b.a[:], MODR[row:row + 1, 0:D].partition_broadcast(128))
                if bk == "ctx":
                    for sb_ in range(2):
                        load(mTp, mTp.a[:, :, sb_ * 256:(sb_ + 1) * 256], MT[sb_, :, 0:256].rearrange("(c p) t -> p c t", p=128))
                else:
                    t0 = CTX + bh * 1024
                    load(mTp, mTp.a[:, :, 0:1024], MT[bb, :, t0:t0 + 1024].rearrange("(c p) t -> p c t", p=128))

                def xsrc(T_c, T_x, t):
                    if bk == "ctx":
                        return T_c[t // 2, (t % 2) * 128:(t % 2 + 1) * 128, :]
                    return T_x[bb, bh * 1024 + t * 128: bh * 1024 + (t + 1) * 128, :]
                for t in range(ntile):
                    load(xb[t], xb[t].a[:], xsrc(ccur, xcur, t))
                groups = [(w_out[l], 0, 16, cg * 256, 256, cg) for cg in range(8)]

                def body(wt, g):
                    cg = g[5]
                    for t in range(ntile):
                        p = pb.next()
                        for k in range(16):
                            P(lambda t=t, k=k: mm(p.a[:, 0:256], mTp.a[:, k, t * 128:(t + 1) * 128], wt.a[:, k, :], k == 0, k == 15), [wt, mTp], [p])
                        tm = tmp_r.next()
                        V(lambda: nc.vector.tensor_tensor(out=tm.a[:], in0=p.a[:, 0:256], in1=g1b.a[:, cg * 256:(cg + 1) * 256], op=ALU.mult), [p, g1b], [tm])
                        V(lambda t=t: nc.vector.scalar_tensor_tensor(out=xb[t].a[:, cg * 256:(cg + 1) * 256], in0=xb[t].a[:, cg * 256:(cg + 1) * 256], scalar=float(ALPHA),
                                                                     in1=tm.a[:], op0=ALU.mult, op1=ALU.add), [xb[t], tm], [xb[t]])
                run_groups(ws, groups, body)
                for t0_ in range(0, ntile, 4):
                    interleave([ln_affine_g(xb[t], lng, lnb, xsrc(C1, X1, t)) for t in range(t0_, t0_ + 4)])
        S.barrier()
        if stop_after == "F1":
            break

        with ExitStack() as es:
            ws = WStream(es, 22, 256, "w")
            groups = []
            for gi in range(22):
                groups.append((w_g[l], 0, 16, gi * 256, 256, WGU[0, gi]))
                groups.append((w_u[l], 0, 16, gi * 256, 256, WGU[1, gi]))
            for cg in range(8):
                for hf in range(2):
                    groups.append((w_d[l], hf * 22 * 128, 22, cg * 256, 256, WD[cg * 2 + hf]))
            cast_i = [0]
            wcb = Rot([sbt(es, "wcb%d" % i, [128, 22 * 256], BF16) for i in range(3)])

            def body(wt, g):
                nk = g[2]
                b_ = wcb.next()
                n = nk * 256
                src = wt.a[:, 0:nk, :].rearrange("p k n -> p (k n)")
                cast_i[0] += 1
                e = cast_i[0] % 2
                if e == 0:
                    V(lambda: nc.vector.tensor_copy(out=b_.a[:, 0:n], in_=src), [wt], [b_])
                elif e == 1:
                    A(lambda: nc.scalar.copy(out=b_.a[:, 0:n], in_=src), [wt], [b_])
                else:
                    G(lambda: nc.gpsimd.tensor_copy(out=b_.a[:, 0:n], in_=src), [wt], [b_])
                store(b_, g[5][:, 0:n], b_.a[:, 0:n])
            run_groups(ws, groups, body, cast=False)
        S.barrier()

        with ExitStack() as es:
            wsb = Rot([sbt(es, "wsb%d" % i, [128, 22 * 256], BF16) for i in range(4)])
            h2T = sbt(es, "h2T", [128, 16, 512], BF16)
            aT = sbt(es, "aT", [128, 44, 512], BF16)
            xb = [sbt(es, "xg%d" % i, [128, D], F32) for i in range(4)]
            xn = Rot([sbt(es, "xng%d" % i, [128, D], F32) for i in range(4)])
            g2b = sbt(es, "g2b", [128, D], F32)
            lng = sbt(es, "lng2", [128, D], F32)
            lnb = sbt(es, "lnb2", [128, D], F32)
            load(lng, lng.a[:], vecs_in[l, 2:3, :].partition_broadcast(128))
            load(lnb, lnb.a[:], vecs_in[l, 3:4, :].partition_broadcast(128))
            tmp_r = Rot([sbt(es, "tmpg%d" % i, [128, 512], F32) for i in range(2)])
            pgu_all = [pst(es, "pgu%d" % i, [128, 512]) for i in range(4)]
            pdn_l = [pst(es, "pdn%d" % i, [128, 512]) for i in range(4)]
            pdn = Rot(pdn_l)
            ptp = Rot(pdn_l[0:2])
            blocks = [("lat", b, h) for b in range(BPC) for h in range(4)]
            if not last:
                blocks = [("ctx", 0, 0)] + blocks
            for (bk, bb, bh) in blocks:
                row = 2 if bk == "ctx" else bb
                load(g2b, g2b.a[:], MODR[row:row + 1, D:2 * D].partition_broadcast(128))

                def xsrc(T_c, T_x, t):
                    if bk == "ctx":
                        return T_c[t // 2, (t % 2) * 128:(t % 2 + 1) * 128, :]
                    return T_x[bb, bh * 512 + t * 128: bh * 512 + (t + 1) * 128, :]
                def f2_tile(t):
                    load(xb[t], xb[t].a[:], xsrc(C1, X1, t))
                    yield
                    yield from modulate_g((xn, ptp), xb[t].a[:], xb[t], h2T, t * 128, row, 2, 3)
                interleave([f2_tile(t) for t in range(4)])
                groups = []
                for gi in range(22):
                    groups.append((WGU[0, gi], 16 * 256, ("g", gi)))
                    groups.append((WGU[1, gi], 16 * 256, ("u", gi)))
                for cg in range(8):
                    for hf in range(2):
                        groups.append((WD[cg * 2 + hf], 22 * 256, ("d", cg, hf)))
                dn_banks = {}
                gt_hold = {}

                def issue(g):
                    t_ = wsb.next()
                    load(t_, t_.a[:, 0:g[1]], g[0][:, 0:g[1]])
                    return t_

                def body(wt, g):
                    pl = g[2]
                    if pl[0] == "g":
                        gt_hold[0] = wt
                        return
                    if pl[0] == "u":
                        wg_ = gt_hold[0]
                        for jj in range(2):
                            j = pl[1] * 2 + jj
                            pgu = pgu_all[0:2] if j % 2 == 0 else pgu_all[2:4]
                            for (w__, p) in ((wg_, pgu[0]), (wt, pgu[1])):
                                wv = w__.a[:, 0:4096].rearrange("p (k n) -> p k n", n=256)
                                for k in range(16):
                                    P(lambda k=k, wv=wv, p=p: mm(p.a[:, :], wv[:, k, jj * 128:(jj + 1) * 128], h2T.a[:, k, :], k == 0, k == 15), [w__, h2T], [p])
                            tm = tmp_r.next()
                            A(lambda: nc.scalar.activation(out=tm.a[:], in_=pgu[0].a[:, :], func=AF.Silu), [pgu[0]], [tm])
                            V(lambda j=j: nc.vector.tensor_tensor(out=aT.a[:, j, :], in0=pgu[1].a[:, :], in1=tm.a[:], op=ALU.mult), [pgu[1], tm], [aT])
                    else:
                        _, cg, hf = pl
                        wv = wt.a[:, 0:22 * 256].rearrange("p (k n) -> p k n", n=256)
                        if hf == 0:
                            dn_banks[cg] = [pdn.next() for _ in range(4)]
                        for t in range(4):
                            p = dn_banks[cg][t]
                            for k in range(22):
                                P(lambda t=t, k=k, p=p: mm(p.a[:, 0:256], aT.a[:, hf * 22 + k, t * 128:(t + 1) * 128], wv[:, k, :], hf == 0 and k == 0, hf == 1 and k == 21),
                                  [wt, aT], [p])
                            if hf == 1:
                                tm = tmp_r.next()
                                V(lambda p=p: nc.vector.tensor_tensor(out=tm.a[:, 0:256], in0=p.a[:, 0:256], in1=g2b.a[:, cg * 256:(cg + 1) * 256], op=ALU.mult), [p, g2b], [tm])
                                V(lambda t=t: nc.vector.scalar_tensor_tensor(out=xb[t].a[:, cg * 256:(cg + 1) * 256], in0=xb[t].a[:, cg * 256:(cg + 1) * 256], scalar=float(ALPHA),
                                                                             in1=tm.a[:, 0:256], op0=ALU.mult, op1=ALU.add), [xb[t], tm], [xb[t]])
                q = [issue(groups[0]), issue(groups[1])]
                for gi_, g in enumerate(groups):
                    if gi_ + 2 < len(groups):
                        q.append(issue(groups[gi_ + 2]))
                    body(q.pop(0), g)
                interleave([ln_affine_g(xb[t], lng, lnb, xsrc(CS, xout, t)) for t in range(4)])
        S.barrier()

    es0.close()
    return nc


_CONST_CACHE = {}


def _constants():
    if _CONST_CACHE:
        return _CONST_CACHE
    import ml_dtypes
    bf = ml_dtypes.bfloat16
    c = {}
    c["ident"] = np.eye(128, dtype=np.float32)
    d = np.arange(128)
    half = (d % 64) // 32
    partner = np.where(half == 0, d + 32, d - 32)
    perm = np.zeros((128, 128), np.float32)
    perm[partner, d] = 1.0
    c["permrope"] = perm
    t = np.arange(SEQ)
    row = (t // GRID_W).astype(np.float32)
    col = (t % GRID_W).astype(np.float32)
    inv = (np.float32(10000.0) ** (-np.arange(32, dtype=np.float32) / np.float32(32))).astype(np.float32)
    pos = np.where((d // 64)[:, None] == 0, row[None, :], col[None, :]).astype(np.float32)
    ang = (pos * inv[d % 32][:, None]).astype(np.float32)
    c["ropecos"] = np.cos(ang).astype(np.float32)
    c["ropesin"] = (np.sin(ang) * np.where(half == 0, -1.0, 1.0)[:, None]).astype(np.float32)
    k = np.arange(128)
    K_, L_ = np.meshgrid(k, k, indexing="ij")
    masks = np.stack([K_ <= L_, K_ > L_, K_ >= L_, K_ < L_, np.ones_like(K_, bool), L_ >= K_, L_ <= K_], axis=1)
    c["masks"] = np.ascontiguousarray(masks.astype(np.float32))
    ch = np.arange(256)
    a256 = 2.0 * np.pi * ((ch[:, None] * ch[None, :]) % 256) / 256.0
    C256, S256 = np.cos(a256), np.sin(a256)
    ccsc = np.concatenate([C256, S256], axis=1).reshape(2, 128, 512).transpose(1, 0, 2)
    c256 = np.concatenate([C256, -S256], axis=1).reshape(2, 128, 512).transpose(1, 0, 2)
    c["ccsc"] = np.ascontiguousarray(ccsc).astype(bf)
    c["c256"] = np.ascontiguousarray(c256).astype(bf)
    s = np.arange(SEQ)
    a2k = 2.0 * np.pi * ((s[:, None] * s[None, :]) % SEQ) / float(SEQ)
    Cs, Ss = np.cos(a2k).astype(np.float32), np.sin(a2k).astype(np.float32)
    dft = np.empty((4, 128, 16, 1024), np.float32)
    for sb_ in range(4):
        dft[sb_, :, :, 0:512] = Cs[:, sb_ * 512:(sb_ + 1) * 512].reshape(16, 128, 512).transpose(1, 0, 2)
        dft[sb_, :, :, 512:1024] = -Ss[:, sb_ * 512:(sb_ + 1) * 512].reshape(16, 128, 512).transpose(1, 0, 2)
    c["dft"] = dft.astype(bf)
    _CONST_CACHE.update(c)
    return _CONST_CACHE


def _nabias(rpb):
    L = rpb.shape[0]
    out = np.full((L, NHEAD, 128, 5, 576), -30000.0, np.float32)
    qc = np.arange(64)
    kc = np.arange(64)
    c_start = np.clip(qc - 8, 0, 48)
    in_col = (kc[None, :] >= c_start[:, None]) & (kc[None, :] < c_start[:, None] + 16)
    dc = np.clip(kc[None, :] - qc[:, None], -15, 15) + 15
    types = [(0, 0, 9), (1, 0, 9), (2, 0, 9), (14, 24, 8), (15, 24, 8)]
    for ty, (j, start, nrows) in enumerate(types):
        for qr in range(2):
            q_abs = 2 * j + qr
            rs = int(np.clip(q_abs - 4, 0, 24))
            for kr in range(nrows):
                k_abs = start + kr
                if not (rs <= k_abs < rs + 8):
                    continue
                dr = k_abs - q_abs + 7
                vals = rpb[:, :, dr, :][:, :, dc]
                vals = np.where(in_col[None, None], vals, np.float32(-30000.0))
                out[:, :, qr * 64:(qr + 1) * 64, ty, kr * 64:(kr + 1) * 64] = vals
    return out


def _prep_shared(inputs, nlayers=DEPTH, l0=0):
    f = lambda k: np.asarray(inputs[k], np.float32)[l0:l0 + nlayers]
    sh = dict(_constants())
    sh["nabias"] = _nabias(f("na_rpb"))
    cw = f("ssm_conv_w")
    sh["convw"] = np.ascontiguousarray(cw.reshape(nlayers, 5, 24, 128).transpose(0, 3, 2, 1))
    sh["convb"] = np.ascontiguousarray(f("ssm_conv_b").reshape(nlayers, 24, 128).transpose(0, 2, 1))
    vecs = np.zeros((nlayers, 8, D), np.float32)
    vecs[:, 0] = f("ln1_g"); vecs[:, 1] = f("ln1_b"); vecs[:, 2] = f("ln2_g"); vecs[:, 3] = f("ln2_b")
    vecs[:, 4] = f("ssm_norm_w")
    vecs[:, 5, 0:64] = f("ssm_dt_bias").reshape(nlayers, 64)
    vecs[:, 5, 64:128] = f("ssm_a_log").reshape(nlayers, 64)
    vecs[:, 5, 128:160] = f("ssm_d")
    sh["vecs"] = vecs
    sh["w_mod"] = np.ascontiguousarray(f("w_mod"))
    sh["w_in"] = np.ascontiguousarray(f("w_in"))
    sh["w_br"] = np.concatenate([f("w_br_na"), f("w_br_fn"), f("w_br_ssm")], axis=1)
    sh["w_out"] = np.ascontiguousarray(f("w_out"))
    sh["w_ffn_gate"] = np.ascontiguousarray(f("w_ffn_gate"))
    sh["w_ffn_up"] = np.ascontiguousarray(f("w_ffn_up"))
    sh["w_ffn_down"] = np.ascontiguousarray(f("w_ffn_down"))
    return sh


def _prep_core(inputs, core, shared):
    b0 = core * BPC
    c = np.asarray(inputs["c"], np.float32)
    c_ctx = np.asarray(inputs["c_ctx"], np.float32)
    cin = np.stack([c[b0], c[b0 + 1], c_ctx], axis=0)
    m = dict(shared)
    m["x"] = np.ascontiguousarray(np.asarray(inputs["x"], np.float32)[b0:b0 + BPC])
    m["ctx"] = np.ascontiguousarray(np.asarray(inputs["ctx"], np.float32)[b0:b0 + BPC])
    m["cT"] = np.ascontiguousarray(cin.T.reshape(16, 128, 3).transpose(1, 0, 2))
    return m


def kernel(**inputs):
    nc = build(DEPTH)
    shared = _prep_shared(inputs)
    in_maps = [_prep_core(inputs, core, shared) for core in range(NCORES)]
    res = run_bass_kernel_spmd(nc, in_maps, core_ids=list(range(NCORES)))
    out = np.concatenate([np.asarray(r["out"], np.float32) for r in res.results], axis=0)
    return out
```
